# Optimizing a Trainium2 kernel written in Bass

```python
import jax, jax.numpy as jnp
from jax import lax
import numpy as np

D_MODEL = 1024
BATCH = 8
SEQ = 4096
DEPTH = 1

CHUNK = 64
GLA_HEADS = 4
GLA_DK = 64
GLA_DV = 128
GLA_GATE_RANK = 16
GLA_GATE_NORMALIZER = 16.0
GLA_NORM_EPS = 1e-5
RWKV_HEADS = 8
RWKV_HEAD = 64
RWKV_W_LORA = 64
RWKV_A_LORA = 64
RWKV_G_LORA = 128
RWKV_GN_EPS = 64e-5
L2_EPS = 1e-12
GLA_WIDTH = GLA_HEADS * GLA_DV
RWKV_WIDTH = RWKV_HEADS * RWKV_HEAD
BRANCH_WIDTH = 512
N_BRANCH = 2
D_FF = 4 * D_MODEL
LN_EPS = 1e-5
DEEPNORM_ALPHA = (2.0 * DEPTH) ** 0.25
DEEPNORM_BETA = (8.0 * DEPTH) ** -0.25
GLA_COLS = (GLA_HEADS * GLA_DK, GLA_HEADS * GLA_DK, GLA_WIDTH, GLA_WIDTH, GLA_GATE_RANK)
RWKV_COLS = (RWKV_WIDTH, RWKV_WIDTH, RWKV_WIDTH, RWKV_W_LORA, RWKV_A_LORA, RWKV_G_LORA)
GLA_IN = 2 * GLA_HEADS * GLA_DK + 2 * GLA_WIDTH + GLA_GATE_RANK
RWKV_IN = 3 * RWKV_WIDTH + RWKV_W_LORA + RWKV_A_LORA + RWKV_G_LORA
IN_WIDTH = GLA_IN + RWKV_IN

kernel_name = 'gla_rwkv7_gated_hybrid_deepnorm'


def _split(u, sizes):
    return jnp.split(u, np.cumsum(sizes)[:-1].tolist(), axis=-1)


def _layernorm(h, g, b, dtype):
    hf = h.astype(jnp.float32)
    mu = jnp.mean(hf, -1, keepdims=True)
    var = jnp.mean(jnp.square(hf - mu), -1, keepdims=True)
    return ((hf - mu) * lax.rsqrt(var + LN_EPS) * g + b).astype(dtype)


def _token_shift(u, mu):
    prev = jnp.pad(u, ((0, 0), (1, 0), (0, 0)))[:, :-1]
    return u + (prev - u) * mu


def _gla_chunk_step(state, inp):
    q, k, v, g = inp
    b = jnp.cumsum(g, axis=2)
    causal = jnp.tril(jnp.ones((CHUNK, CHUNK), dtype=bool))
    diff = b[:, :, :, None, :] - b[:, :, None, :, :]
    decay = jnp.exp(jnp.where(causal[None, None, :, :, None], diff, -jnp.inf))
    scores = jnp.einsum('bhid,bhjd,bhijd->bhij', q, k, decay)
    o = (jnp.einsum('bhij,bhjv->bhiv', scores, v)
         + jnp.einsum('bhid,bhdv->bhiv', q * jnp.exp(b), state))
    b_last = b[:, :, -1, :]
    state = (state * jnp.exp(b_last)[..., None]
             + jnp.einsum('bhjd,bhjv->bhdv', k * jnp.exp(b_last[:, :, None, :] - b), v))
    return state, o


def _gla_branch(h, w_gk_up, b_gk, gla_norm_w):
    Bsz, S, _ = h.shape
    q, k, v, g_out, gk_low = _split(h, GLA_COLS)
    gk = jax.nn.log_sigmoid((gk_low @ w_gk_up + b_gk).astype(jnp.float32)) / GLA_GATE_NORMALIZER
    n_chunks = S // CHUNK

    def to_chunks(t, d):
        return t.astype(jnp.float32).reshape(Bsz, n_chunks, CHUNK, GLA_HEADS, d).transpose(1, 0, 3, 2, 4)

    state0 = jnp.zeros((Bsz, GLA_HEADS, GLA_DK, GLA_DV), jnp.float32)
    _, o = lax.scan(_gla_chunk_step, state0,
                    (to_chunks(q * GLA_DK ** -0.5, GLA_DK), to_chunks(k, GLA_DK),
                     to_chunks(v, GLA_DV), to_chunks(gk, GLA_DK)))
    o = o.transpose(1, 0, 3, 2, 4).reshape(Bsz, S, GLA_HEADS, GLA_DV)
    o = o * lax.rsqrt(jnp.mean(jnp.square(o), -1, keepdims=True) + GLA_NORM_EPS) * gla_norm_w
    return o.reshape(Bsz, S, GLA_WIDTH) * jax.nn.silu(g_out.astype(jnp.float32))


def _rwkv7_step(state, inp):
    r, w, k, v, kk, kb = inp
    sa = -jnp.einsum('bhvk,bhk->bhv', state, kk)
    state = state * w[:, :, None, :] + sa[..., None] * kb[:, :, None, :] + v[..., None] * k[:, :, None, :]
    y = jnp.einsum('bhvk,bhk->bhv', state, r)
    return state, y


def _rwkv7_branch(h, mu_shift, w0, w_up, a0, a_up, g_up, k_k, k_a, r_k, gn_w, gn_b):
    Bsz, S, _ = h.shape
    u = _token_shift(h, mu_shift).astype(jnp.float32)
    r, k, v, w_low, a_low, g_low = _split(u, RWKV_COLS)
    w_log = -jax.nn.softplus(-(w0 + jnp.tanh(w_low) @ w_up)) - 0.5
    decay = jnp.exp(-jnp.exp(w_log))
    a = jax.nn.sigmoid(a0 + a_low @ a_up)
    g = jax.nn.sigmoid(g_low) @ g_up

    def heads(t):
        return t.reshape(Bsz, S, RWKV_HEADS, RWKV_HEAD)

    kk = heads(k * k_k)
    kk = kk / jnp.maximum(jnp.sqrt(jnp.sum(jnp.square(kk), -1, keepdims=True)), L2_EPS)
    k = k * (1.0 + (a - 1.0) * k_a)
    r_h, k_h, v_h, a_h, w_h = heads(r), heads(k), heads(v), heads(a), heads(decay)

    def tm(t):
        return jnp.swapaxes(t, 0, 1)

    state0 = jnp.zeros((Bsz, RWKV_HEADS, RWKV_HEAD, RWKV_HEAD), jnp.float32)
    _, y = lax.scan(_rwkv7_step, state0, (tm(r_h), tm(w_h), tm(k_h), tm(v_h), tm(kk), tm(kk * a_h)))
    y = tm(y)
    mu = jnp.mean(y, -1, keepdims=True)
    var = jnp.mean(jnp.square(y - mu), -1, keepdims=True)
    y = ((y - mu) * lax.rsqrt(var + RWKV_GN_EPS)).reshape(Bsz, S, RWKV_WIDTH) * gn_w + gn_b
    bonus = jnp.sum(r_h * k_h * r_k, -1, keepdims=True) * v_h
    return (y + bonus.reshape(Bsz, S, RWKV_WIDTH)) * g


def setup_inputs(seed: int = 0) -> dict:
    key = jax.random.key(seed)
    ks = iter(jax.random.split(key, 32))
    L = DEPTH

    def nrm(shape, scale):
        return scale * jax.random.normal(next(ks), shape, jnp.float32)

    col_scale = np.ones((IN_WIDTH,), np.float32)
    g_off = np.cumsum((0,) + GLA_COLS)
    r_off = GLA_IN + np.cumsum((0,) + RWKV_COLS)
    col_scale[g_off[2]:g_off[3]] = DEEPNORM_BETA
    col_scale[r_off[2]:r_off[3]] = DEEPNORM_BETA
    n = np.arange(RWKV_WIDTH, dtype=np.float32)
    decay_speed = (-7.0 + 5.0 * (n / (RWKV_WIDTH - 1)) ** 0.85 + 0.5).astype(np.float32)

    x = nrm((BATCH, SEQ, D_MODEL), 1.0)
    w_in = nrm((L, D_MODEL, IN_WIDTH), D_MODEL ** -0.5) * jnp.asarray(col_scale)
    mu_shift = jax.random.uniform(next(ks), (L, RWKV_IN), jnp.float32)
    w_gk_up = nrm((L, GLA_GATE_RANK, GLA_HEADS * GLA_DK), GLA_GATE_RANK ** -0.5)
    b_gk = nrm((L, GLA_HEADS * GLA_DK), 0.1)
    gla_norm_w = 1.0 + nrm((L, GLA_DV), 0.02)
    rwkv_w0 = jnp.asarray(decay_speed) + nrm((L, RWKV_WIDTH), 0.1)
    rwkv_w_up = nrm((L, RWKV_W_LORA, RWKV_WIDTH), 0.1 * RWKV_W_LORA ** -0.5)
    rwkv_a0 = nrm((L, RWKV_WIDTH), 0.1)
    rwkv_a_up = nrm((L, RWKV_A_LORA, RWKV_WIDTH), 0.5 * RWKV_A_LORA ** -0.5)
    rwkv_g_up = nrm((L, RWKV_G_LORA, RWKV_WIDTH), RWKV_G_LORA ** -0.5)
    rwkv_k_k = 0.85 + nrm((L, RWKV_WIDTH), 0.02)
    rwkv_k_a = 1.0 + nrm((L, RWKV_WIDTH), 0.02)
    rwkv_r_k = -0.04 + nrm((L, RWKV_HEADS, RWKV_HEAD), 0.02)
    rwkv_gn_w = 1.0 + nrm((L, RWKV_WIDTH), 0.02)
    rwkv_gn_b = nrm((L, RWKV_WIDTH), 0.02)
    w_merge = nrm((L, D_MODEL, N_BRANCH * D_MODEL), D_MODEL ** -0.5)
    b_merge = nrm((L, N_BRANCH * D_MODEL), 0.02)
    w_branch = nrm((L, N_BRANCH, BRANCH_WIDTH, D_MODEL), DEEPNORM_BETA * BRANCH_WIDTH ** -0.5)
    w_out = nrm((L, D_MODEL, D_MODEL), DEEPNORM_BETA * D_MODEL ** -0.5)
    ln1_g = 1.0 + nrm((L, D_MODEL), 0.02)
    ln1_b = nrm((L, D_MODEL), 0.02)
    w_mlp_up = nrm((L, D_MODEL, D_FF), DEEPNORM_BETA * D_MODEL ** -0.5)
    b_mlp_up = nrm((L, D_FF), 0.02)
    w_mlp_down = nrm((L, D_FF, D_MODEL), DEEPNORM_BETA * D_FF ** -0.5)
    b_mlp_down = nrm((L, D_MODEL), 0.02)
    ln2_g = 1.0 + nrm((L, D_MODEL), 0.02)
    ln2_b = nrm((L, D_MODEL), 0.02)
    return {'x': x, 'w_in': w_in, 'mu_shift': mu_shift, 'w_gk_up': w_gk_up, 'b_gk': b_gk,
            'gla_norm_w': gla_norm_w, 'rwkv_w0': rwkv_w0, 'rwkv_w_up': rwkv_w_up,
            'rwkv_a0': rwkv_a0, 'rwkv_a_up': rwkv_a_up, 'rwkv_g_up': rwkv_g_up,
            'rwkv_k_k': rwkv_k_k, 'rwkv_k_a': rwkv_k_a, 'rwkv_r_k': rwkv_r_k,
            'rwkv_gn_w': rwkv_gn_w, 'rwkv_gn_b': rwkv_gn_b, 'w_merge': w_merge, 'b_merge': b_merge,
            'w_branch': w_branch, 'w_out': w_out, 'ln1_g': ln1_g, 'ln1_b': ln1_b,
            'w_mlp_up': w_mlp_up, 'b_mlp_up': b_mlp_up, 'w_mlp_down': w_mlp_down,
            'b_mlp_down': b_mlp_down, 'ln2_g': ln2_g, 'ln2_b': ln2_b}


def reference(x, w_in, mu_shift, w_gk_up, b_gk, gla_norm_w, rwkv_w0, rwkv_w_up, rwkv_a0,
              rwkv_a_up, rwkv_g_up, rwkv_k_k, rwkv_k_a, rwkv_r_k, rwkv_gn_w, rwkv_gn_b,
              w_merge, b_merge, w_branch, w_out, ln1_g, ln1_b, w_mlp_up, b_mlp_up,
              w_mlp_down, b_mlp_down, ln2_g, ln2_b):
    dt = x.dtype
    Bsz, S, _ = x.shape
    for l in range(DEPTH):
        h = x @ w_in[l]
        o_a = _gla_branch(h[..., :GLA_IN], w_gk_up[l], b_gk[l], gla_norm_w[l])
        o_b = _rwkv7_branch(h[..., GLA_IN:], mu_shift[l], rwkv_w0[l], rwkv_w_up[l], rwkv_a0[l],
                            rwkv_a_up[l], rwkv_g_up[l], rwkv_k_k[l], rwkv_k_a[l], rwkv_r_k[l],
                            rwkv_gn_w[l], rwkv_gn_b[l])
        branches = jnp.stack([o_a, o_b], axis=2).astype(dt)
        y = jnp.einsum('bsnc,ncd->bsnd', branches, w_branch[l])
        gates = jax.nn.sigmoid((x @ w_merge[l] + b_merge[l]).astype(jnp.float32))
        gates = gates.reshape(Bsz, S, N_BRANCH, D_MODEL)
        mix = jnp.sum(gates * y, axis=2).astype(dt) @ w_out[l]
        x = _layernorm(DEEPNORM_ALPHA * x + mix, ln1_g[l], ln1_b[l], dt)
        ffn = jnp.square(jax.nn.relu(x @ w_mlp_up[l] + b_mlp_up[l])) @ w_mlp_down[l] + b_mlp_down[l]
        x = _layernorm(DEEPNORM_ALPHA * x + ffn, ln2_g[l], ln2_b[l], dt)
    return x
```

```python
import contextlib
import numpy as np
import concourse.bass as bass
import concourse.mybir as mybir
from concourse.bass_utils import run_bass_kernel_spmd

F32 = mybir.dt.float32
BF16 = mybir.dt.bfloat16
AF = mybir.ActivationFunctionType
ALU = mybir.AluOpType
AX = mybir.AxisListType

D = 1024
KC = 8
DFF = 4096
INW = 3344
C0 = float(np.exp(-0.5))
ALPHA = float(2.0 ** 0.25)
PP_MU, PP_BM, PP_B1, PP_W0, PP_A0, PP_KK, PP_KA, PP_RK, PP_GNW, PP_GNB, PP_BGK, NPP = 0, 14, 30, 62, 66, 70, 74, 78, 82, 86, 90, 92
R_LN1G, R_LN1B, R_LN2G, R_LN2B, R_B2, R_GNW, NR = 0, 1024, 2048, 3072, 4096, 5120, 5248

ENGS = ("pe", "act", "dve", "pool", "sp")


class Prog:
    def __init__(self, nc, stack, same_engine_sync=True):
        self.nc = nc
        self.stack = stack
        self.q = {e: [] for e in ENGS}
        self.cnt = {e: 0 for e in ENGS}
        self.dcnt = {}
        self.sems = {}
        self.waited = {e: {} for e in ENGS}
        self.lastw = {}
        self.readers = {}
        self.same = same_engine_sync
        self.n_ps = 0
        self.nops = 0
        self.max_ops = None
        self.marks = []
        self.sb_bytes = {}

    def sem(self, name):
        if name not in self.sems:
            self.sems[name] = self.stack.enter_context(self.nc.semaphore("s_" + name))
        return self.sems[name]

    def _deps(self, eng, reads, writes):
        need = {}

        def add(ev):
            if ev is None:
                return
            k, v = ev
            if k == eng and (eng == "pe" or (not self.same and eng in ("act", "dve"))):
                return
            if self.waited[eng].get(k, 0) >= v:
                return
            if need.get(k, 0) < v:
                need[k] = v

        for r in reads:
            add(self.lastw.get(r))
        for w in writes:
            add(self.lastw.get(w))
            for k, v in self.readers.get(w, {}).items():
                add((k, v))
        for k, v in need.items():
            self.waited[eng][k] = v
        return list(need.items())

    def _record(self, ev, reads, writes):
        for w in writes:
            self.lastw[w] = ev
            self.readers[w] = {}
        for r in reads:
            d = self.readers.setdefault(r, {})
            if d.get(ev[0], 0) < ev[1]:
                d[ev[0]] = ev[1]

    def op(self, eng, fn, reads=(), writes=(), inc=True):
        self.nops += 1
        if self.max_ops is not None and self.nops > self.max_ops:
            return
        waits = self._deps(eng, reads, writes)
        if inc:
            self.cnt[eng] += 1
            ev = (eng, self.cnt[eng])
        else:
            ev = (eng, self.cnt[eng] + 1)
        self.q[eng].append((waits, fn, eng if inc else None, 1))
        self._record(ev, reads, writes)

    def dma(self, eng, dsem, out, in_, reads=(), writes=(), **kw):
        self.nops += 1
        if self.max_ops is not None and self.nops > self.max_ops:
            return
        waits = self._deps(eng, reads, writes)
        self.dcnt[dsem] = self.dcnt.get(dsem, 0) + 16
        ev = (dsem, self.dcnt[dsem])
        self.q[eng].append((waits, lambda e: e.dma_start(out=out, in_=in_, **kw), dsem, 16))
        self._record(ev, reads, writes)

    def mark(self, name):
        self.marks.append((name, self.nops))

    def barrier(self):
        allv = [(k, v) for k, v in list(self.cnt.items()) + list(self.dcnt.items()) if v > 0]
        for eng in ENGS:
            waits = []
            for k, v in allv:
                if self.waited[eng].get(k, 0) < v:
                    self.waited[eng][k] = v
                    waits.append((k, v))
            self.q[eng].append((waits, None, None, 0))

    def final_wait(self, eng, dsems):
        waits = [(d, self.dcnt[d]) for d in dsems if d in self.dcnt]
        self.q[eng].append((waits, None, None, 0))

    def emit(self):
        nc = self.nc
        for k in list(self.cnt) + list(self.dcnt):
            self.sem(k)
        with nc.Block() as block:
            def mk(eng):
                def body(e):
                    for waits, fn, isem, inc in self.q[eng]:
                        for k, v in waits:
                            e.wait_ge(self.sems[k], v)
                        if fn is None:
                            continue
                        ins = fn(e)
                        if isem is not None:
                            ins.then_inc(self.sems[isem], inc)
                return body
            block.tensor(mk("pe"))
            block.scalar(mk("act"))
            block.vector(mk("dve"))
            block.gpsimd(mk("pool"))
            block.sync(mk("sp"))


def C(name, *args, **kwargs):
    return lambda e: getattr(e, name)(*args, **kwargs)


def bcast(ap, pos, n):
    pat = [list(x) for x in ap.ap]
    pat.insert(pos, [0, n])
    return bass.AP(ap.tensor, ap.offset, pat)


def build(T, same_engine_sync=True, phases=3, max_ops=None, SW=(2, 1)):
    TB = 256
    NBLK = T // TB
    nc = bass.Bass("TRN2", target_bir_lowering=False)

    def dram(name, shape, dt, kind):
        return nc.dram_tensor(name, shape, dt, kind=kind).ap()

    xT_d = dram("xT", [D, T], F32, "ExternalInput")
    x_d = dram("x", [T, D], F32, "ExternalInput")
    w_in_d = dram("w_in", [D, INW], F32, "ExternalInput")
    w_mg_d = dram("w_merge", [D, 2 * D], F32, "ExternalInput")
    w_br_d = dram("w_branch", [D, D], F32, "ExternalInput")
    w_out_d = dram("w_out", [D, D], F32, "ExternalInput")
    w1_d = dram("w1", [D, DFF], F32, "ExternalInput")
    w2_d = dram("w2", [DFF, D], F32, "ExternalInput")
    pp_d = dram("pp", [128, NPP], F32, "ExternalInput")
    rows_d = dram("rows", [1, NR], F32, "ExternalInput")
    wgk_d = dram("w_gk_up", [16, 256], F32, "ExternalInput")
    waup_d = dram("wa_up", [128, 512], F32, "ExternalInput")
    gup_d = dram("g_up", [128, 512], F32, "ExternalInput")
    out_d = dram("out", [T, D], F32, "ExternalOutput")
    oT_d = dram("oT_scr", [D, T], BF16, "Internal")
    x1_d = dram("x1_scr", [T, D], F32, "Internal")

    xT_v = xT_d.rearrange("(kc p) t -> p kc t", p=128)
    oT_v = oT_d.rearrange("(kc p) t -> p kc t", p=128)
    w_in_v = w_in_d.rearrange("(kc p) n -> p kc n", p=128)

    with contextlib.ExitStack() as st:
        P = Prog(nc, st, same_engine_sync)
        P.max_ops = max_ops
        nc._prog = P

        def sbt(stack, name, shape, dt):
            nb = int(np.prod(shape[1:])) * (2 if dt == BF16 else 4)
            P.sb_bytes[id(stack)] = P.sb_bytes.get(id(stack), 0) + ((nb + 31) // 32) * 32
            return stack.enter_context(nc.sbuf_tensor("sb_" + name, shape, dt))

        banks = [st.enter_context(nc.psum_tensor("psb%d" % i, [128, 512], F32)) for i in range(8)]

        def psum():
            i = P.n_ps % 8
            P.n_ps += 1
            return banks[i], ("ps", i)

        def bfv(bank):
            return bank[:, :].bitcast(BF16)

        pp = sbt(st, "pp", [128, NPP], F32)
        dpar = sbt(st, "dpar", [128, 8], F32)
        ident = sbt(st, "ident", [128, 128], BF16)
        ones_bd = sbt(st, "ones_bd", [128, 128], BF16)
        m_su = sbt(st, "m_su", [128, 128], F32)
        m_iu = sbt(st, "m_iu", [128, 128], F32)
        m_sl = sbt(st, "m_sl", [128, 128], F32)
        cmask = sbt(st, "cmask", [128, 128], F32)
        identf = sbt(st, "identf", [128, 128], F32)
        rm128 = sbt(st, "rm128", [128, TB], F32)
        rm64 = sbt(st, "rm64", [128, TB], F32)
        onesf = sbt(st, "onesf", [128, 128], F32)

        P.dma("sp", "d_pp", pp[:, :], pp_d[:, :], writes=["pp"])
        P.op("pool", C("memset", onesf[:, :], 1.0), writes=["onesf"])
        P.op("pool", C("memset", rm128[:, :], 1.0), writes=["rm128"])
        P.op("pool", C("memset", rm64[:, :], 1.0), writes=["rm64"])
        for t0 in range(0, TB, 128):
            P.op("pool", C("memset", rm128[:, t0:t0 + 1], 0.0), writes=["rm128"])
        for t0 in range(0, TB, 64):
            P.op("pool", C("memset", rm64[:, t0:t0 + 1], 0.0), writes=["rm64"])
        P.op("pool", C("affine_select", out=cmask[:, :], in_=onesf[:, :], pattern=[[1, 128]],
                                               compare_op=ALU.is_ge, fill=0.0, base=0, channel_multiplier=-1),
             reads=["onesf"], writes=["cmask"])
        P.op("pool", C("affine_select", out=identf[:, :], in_=onesf[:, :], pattern=[[1, 128]],
                                               compare_op=ALU.is_equal, fill=0.0, base=0, channel_multiplier=-1),
             reads=["onesf"], writes=["identf"])
        P.op("pool", C("tensor_copy", ident[:, :], identf[:, :]), reads=["identf"], writes=["ident"])
        for (mt, cop, name) in ((m_su, ALU.is_gt, "m_su"), (m_iu, ALU.is_ge, "m_iu")):
            P.op("pool", C("memset", mt[:, :], 0.0), writes=[name])
            for hb in (0, 64):
                P.op("pool", C("affine_select",
                    out=mt[hb:hb + 64, hb:hb + 64], in_=onesf[hb:hb + 64, hb:hb + 64], pattern=[[1, 64]],
                    compare_op=cop, fill=0.0, base=0, channel_multiplier=-1), reads=["onesf"], writes=[name])
        P.op("pool", C("memset", m_sl[:, :], 0.0), writes=["m_sl"])
        for hb in (0, 64):
            P.op("pool", C("affine_select",
                out=m_sl[hb:hb + 64, hb:hb + 64], in_=onesf[hb:hb + 64, hb:hb + 64], pattern=[[-1, 64]],
                compare_op=ALU.is_gt, fill=0.0, base=0, channel_multiplier=1), reads=["onesf"], writes=["m_sl"])
        P.op("pool", C("memset", ones_bd[:, :], 0.0), writes=["ones_bd"])
        for hb in (0, 64):
            P.op("pool", C("memset", ones_bd[hb:hb + 64, hb:hb + 64], 1.0), writes=["ones_bd"])
        P.op("dve", C("tensor_scalar", dpar[:, 0:2], pp[:, PP_BGK:PP_BGK + 2], -1.0, None, ALU.mult),
             reads=["pp"], writes=["dpar"])
        P.op("dve", C("tensor_scalar", dpar[:, 2:6], pp[:, PP_KA:PP_KA + 4], -1.0, 1.0, ALU.mult, ALU.add),
             reads=["pp"], writes=["dpar"])

        def ppc(off, i):
            return pp[:, off + i:off + i + 1]

        def mm_group(out_ap, pairs, pskey, reads, inc_last=True):
            n = len(pairs)
            for i, (l, r) in enumerate(pairs):
                P.op("pe", C("matmul", out_ap, l, r, start=(i == 0), stop=(i == n - 1)),
                     reads=reads, writes=[pskey], inc=(i == n - 1) and inc_last)

        if phases >= 1:
          with contextlib.ExitStack() as sa:
            win = sbt(sa, "win", [128, KC, INW], BF16)
            wgk = sbt(sa, "wgk", [16, 256], F32)
            wupP = sbt(sa, "wupP", [128, 512], BF16)
            aupP = sbt(sa, "aupP", [128, 512], BF16)
            gup = sbt(sa, "gup", [128, 512], BF16)
            gnw_bc = sbt(sa, "gnw_bc", [128, 128], F32)
            xTb = [sbt(sa, "xTb%d" % i, [128, KC, TB], BF16) for i in range(2)]
            gklT = sbt(sa, "gklT", [16, TB], F32)
            lsp = sbt(sa, "lsp", [128, 2, TB], F32)
            cum = sbt(sa, "cum", [128, 2, TB], F32)
            eb = lsp
            enb = cum
            qbd = sbt(sa, "qbd", [128, 2, 2, TB], BF16)
            ktT = sbt(sa, "ktT", [128, 2, TB], BF16)
            siluT = sbt(sa, "siluT", [128, 4, TB], BF16)
            vtok = sbt(sa, "vtok", [128, TB // 128, 512], BF16)
            ktok = sbt(sa, "ktok", [128, 256], BF16)
            sT = sbt(sa, "sT", [128, 4, 128], BF16)
            gsq = sbt(sa, "gsq", [128, 4, 128], F32)
            gss = sbt(sa, "gss", [128, 4], F32)
            grs = sbt(sa, "grs", [128, 4], F32)
            gon = sbt(sa, "gon", [128, 4, 128], BF16)
            oaT = [sbt(sa, "oaT", [128, 4, TB], BF16)] * 2
            Sg32 = sbt(sa, "Sg32", [128, 2, 128], F32)
            Sgbf = sbt(sa, "Sgbf", [128, 2, 128], BF16)
            Sgt = sbt(sa, "Sgt", [128, 2, 128], F32)
            hb = [sbt(sa, "hb%d" % i, [128, TB + 1], F32) for i in range(2)]
            dtm = [sbt(sa, "dtm%d" % i, [128, TB], F32) for i in range(2)]
            hprev = sbt(sa, "hprev", [128, 14], F32)
            r32 = sbt(sa, "r32", [128, 4, TB], F32)
            k32 = sbt(sa, "k32", [128, 4, TB], F32)
            vbf = [sbt(sa, "vbf%d" % i, [128, 4, TB], BF16) for i in range(2)]
            lora32 = sbt(sa, "lora32", [128, TB], F32)
            glow32 = sbt(sa, "glow32", [128, TB], F32)
            loraT = sbt(sa, "loraT", [128, TB], BF16)
            sgb = sbt(sa, "sgb", [128, TB], BF16)
            tnames = ["sig", "a32", "cs", "csx", "En", "Ep", "Enx", "kkr", "kk", "kp", "lnss", "inv", "tt"]
            tmp = {n: sbt(sa, "t_" + n, [128, TB], F32) for n in tnames}
            sqb = sbt(sa, "sqb", [128, TB], BF16)
            rkb = sbt(sa, "rkb", [128, TB], BF16)
            ARc = [sbt(sa, "ARc%d" % i, [128, 4, 2, TB], BF16) for i in range(2)]
            BKc = [sbt(sa, "BKc%d" % i, [128, 4, 2, TB], BF16) for i in range(2)]
            pass
            pass
            coef = [sbt(sa, "coef%d" % i, [128, 4, TB], BF16) for i in range(2)]
            gT = [sbt(sa, "gT%d" % i, [128, 4, TB], BF16) for i in range(2)]
            wcl = [sbt(sa, "wcl%d" % i, [128, 4, TB // 64], F32) for i in range(2)]
            ynT = [sbt(sa, "ynT%d" % i, [128, 4, TB], BF16) for i in range(2)]
            yraw = sbt(sa, "yraw", [128, 4, 128], BF16)
            ysqb = sbt(sa, "ysqb", [128, TB], BF16)
            ym = sbt(sa, "ym", [128, TB], F32)
            yv = sbt(sa, "yv", [128, TB], F32)
            ARbd = [sbt(sa, "ARbd%d" % i, [128, 4, 2, 128], BF16) for i in range(2)]
            BKbd = [sbt(sa, "BKbd%d" % i, [128, 4, 2, 128], BF16) for i in range(2)]
            VTbd = [sbt(sa, "VTbd%d" % i, [128, 4, 128], BF16) for i in range(2)]
            QX = [[sbt(sa, "QX%d_%d" % (b, i), [128, 4, 2, 128], BF16) for i in range(2)] for b in range(2)]
            Pt = [[sbt(sa, "Pt%d_%d" % (b, i), [128, 4, 128], BF16) for i in range(2)] for b in range(2)]
            Aak = [sbt(sa, "Aak%d" % i, [128, 4, 128], BF16) for i in range(2)]
            Arb = [sbt(sa, "Arb%d" % i, [128, 4, 128], BF16) for i in range(2)]
            Ark = [sbt(sa, "Ark%d" % i, [128, 4, 128], BF16) for i in range(2)]
            Btok = [sbt(sa, "Btok%d" % i, [128, 4, 128], BF16) for i in range(2)]
            Ktok = [sbt(sa, "Ktok%d" % i, [128, 4, 128], BF16) for i in range(2)]
            Vtok = [sbt(sa, "Vtok%d" % i, [128, 4, 128], BF16) for i in range(2)]
            RH0 = sbt(sa, "RH0", [128, 4, 128], BF16)
            Ubf = sbt(sa, "Ubf", [128, 4, 128], BF16)
            S32 = sbt(sa, "S32", [128, 4, 128], F32)
            Sbf = sbt(sa, "Sbf", [128, 4, 128], BF16)
            St = sbt(sa, "St", [128, 4, 128], F32)
            obT = [sbt(sa, "obT", [128, 4, TB], BF16)] * 2
            ot1 = sbt(sa, "ot1", [128, TB], F32)
            ot2 = sbt(sa, "ot2", [128, TB], F32)

            P.mark("wloads")
            groups = [(1536, 1552), (0, 512), (512, 1024), (1024, 1536), (1552, 2064), (2064, 2576),
                      (2576, 3088), (3088, 3344)]

            def wkeys(c0, c1):
                return [("win", g) for g, (a, b) in enumerate(groups) if a < c1 and c0 < b]

            P.dma("pool", "d_wgk", wgk[:, :], wgk_d[:, :], writes=["wgk"])
            for g, (c0, c1) in enumerate(groups):
                P.dma("pool", "d_win%d" % g, win[:, :, c0:c1], w_in_v[:, :, c0:c1], writes=[("win", g)])
                if g == 0:
                    P.dma("pool", "d_xT0", xTb[0][:, :, :], xT_v[:, :, 0:TB], writes=[("xTb", 0)])
            P.op("dve", C("memset", wupP[64:128, :], 0.0), writes=["wupP"])
            P.op("dve", C("memset", aupP[0:64, :], 0.0), writes=["aupP"])
            P.op("dve", C("memset", qbd[:, :, :, :], 0.0), writes=["qbd"])
            P.dma("pool", "d_wup", wupP[0:64, :], waup_d[0:64, :], writes=["wupP"])
            P.dma("pool", "d_aup", aupP[64:128, :], waup_d[64:128, :], writes=["aupP"])
            P.dma("pool", "d_gup", gup[:, :], gup_d[:, :], writes=["gup"])
            P.dma("sp", "d_gnw", gnw_bc[:, :], rows_d[0:1, R_GNW:R_GNW + 128].to_broadcast([128, 128]), writes=["gnw_bc"])

            P.mark("stateinit")
            P.op("dve", C("memset", Sg32[:, :, :], 0.0), writes=["Sg32"])
            P.op("dve", C("memset", Sgbf[:, :, :], 0.0), writes=["Sgbf"])
            P.op("dve", C("memset", S32[:, :, :], 0.0), writes=["S32"])
            P.op("dve", C("memset", Sbf[:, :, :], 0.0), writes=["Sbf"])
            P.op("dve", C("memset", hprev[:, :], 0.0), writes=[("hprev", i) for i in range(14)])
            for i in range(2):
                P.op("dve", C("memset", ARbd[i][:, :, :, :], 0.0), writes=[("ARbd", i)])
                P.op("dve", C("memset", BKbd[i][:, :, :, :], 0.0), writes=[("BKbd", i)])
                P.op("dve", C("memset", VTbd[i][:, :, :], 0.0), writes=[("VTbd", i)])

            def proj_fm(blk, c0, ncols, ):
                xs = blk % 2
                bank, pk = psum()
                outp = bank[0:ncols, 0:TB]
                mm_group(outp, [(win[:, kc, c0:c0 + ncols], xTb[xs][:, kc, :]) for kc in range(KC)], pk,
                         reads=wkeys(c0, c0 + ncols) + [("xTb", xs)])
                return outp, pk

            chunk_ctr = [0]

            def emit_block(blk, part):
                xs = blk % 2
                t0 = blk * TB
                if part == "front_gla":
                    P.mark("gla_gate")
                    o, pk = proj_fm(blk, 1536, 16)
                    P.op("act", C("activation", out=gklT[:, :], in_=o, func=AF.Copy), reads=[pk], writes=["gklT"])
                    bank, pk = psum()
                    for c in range(2):
                        mm_group(bank[:, c * TB:(c + 1) * TB], [(wgk[0:16, c * 128:(c + 1) * 128], gklT[0:16, :])], pk,
                                 reads=["wgk", "gklT"])
                    for c in range(2):
                        P.op("act", C("activation", out=lsp[:, c, :], in_=bank[:, c * TB:(c + 1) * TB],
                                                                             func=AF.Exp, bias=dpar[:, c:c + 1], scale=-1.0),
                             reads=[pk, "dpar"], writes=["lsp"])
                    P.op("act", C("activation", out=lsp[:, :, :], in_=lsp[:, :, :], func=AF.Ln, bias=1.0, scale=1.0),
                         reads=["lsp"], writes=["lsp"])
                    for c in range(2):
                        P.op("dve", C("tensor_tensor_scan", cum[:, c, :], rm128[:, :], lsp[:, c, :], 0.0, ALU.mult, ALU.add),
                             reads=["lsp", "rm128"], writes=["cum"])
                    P.op("act", C("activation", out=eb[:, :, :], in_=cum[:, :, :], func=AF.Exp, scale=-1.0 / 16.0),
                         reads=["cum"], writes=["lsp"])
                    P.op("act", C("activation", out=enb[:, :, :], in_=cum[:, :, :], func=AF.Exp, scale=1.0 / 16.0),
                         reads=["cum"], writes=["cum"])
                    yield
                    P.mark("gla_qk")
                    for c in range(2):
                        o, pk = proj_fm(blk, c * 128, 128)
                        for hh in range(2):
                            hr = slice(hh * 64, hh * 64 + 64)
                            P.op("dve", C("scalar_tensor_tensor", out=qbd[hr, c, hh, :], in0=o[hr, :], scalar=0.125, in1=eb[hr, c, :],
                                          op0=ALU.mult, op1=ALU.mult), reads=[pk, "lsp"], writes=["qbd"])
                    for c in range(2):
                        o, pk = proj_fm(blk, 256 + c * 128, 128)
                        P.op("dve", C("tensor_tensor", out=ktT[:, c, :], in0=o, in1=enb[:, c, :], op=ALU.mult),
                             reads=[pk, "cum"], writes=["ktT"])
                        yield
                    P.mark("gla_silu")
                    for c in range(4):
                        o, pk = proj_fm(blk, 1024 + c * 128, 128)
                        P.op("act", C("activation", out=siluT[:, c, :], in_=o, func=AF.Silu),
                             reads=[pk], writes=["siluT"])
                        yield
                    P.mark("gla_v")
                    for j in range(TB // 128):
                        bank, pk = psum()
                        mm_group(bank[:, :], [(xTb[xs][:, kc, j * 128:(j + 1) * 128], win[:, kc, 512:1024]) for kc in range(KC)], pk,
                                 reads=wkeys(512, 1024) + [("xTb", xs)])
                        P.op("act", C("activation", out=vtok[:, j, :], in_=bank[:, :], func=AF.Copy),
                             reads=[pk], writes=[("vtok", j)])
                        yield

                if part == "gla_core":
                    P.mark("gla_core")
                    for j in range(TB // 128):
                        ts = slice(j * 128, (j + 1) * 128)
                        bank, pk = psum()
                        bv = bfv(bank)
                        for c in range(2):
                            P.op("pe", C("transpose", bv[:, c * 128:(c + 1) * 128], ktT[:, c, ts], ident[:, :]),
                                 reads=["ktT", "ident"], writes=[pk], inc=(c == 1))
                        P.op("dve", C("tensor_copy", ktok[:, :], bv[:, 0:256]), reads=[pk], writes=["ktok"])
                        yield
                        bank_s, pks = psum()
                        for c in range(2):
                            P.op("pe", C("matmul", bank_s[:, c * 256:(c + 1) * 256], ktT[:, c, ts], qbd[:, c, :, ts], start=True, stop=True),
                                 reads=["ktT", "qbd"], writes=[pks], inc=(c == 1))
                        P.op("dve", C("tensor_tensor",
                            out=sT[:, :, :], in0=bank_s[:, :].rearrange("p (h i) -> p h i", h=4), in1=bcast(cmask[:, :], 1, 4), op=ALU.mult),
                            reads=[pks, "cmask"], writes=["sT"])
                        yield
                        bank_o, pko = psum()
                        for h in range(4):
                            hr = slice((h % 2) * 64, (h % 2) * 64 + 64)
                            c = h // 2
                            P.op("pe", C("matmul",
                                bank_o[:, h * 128:(h + 1) * 128], sT[:, h, :], vtok[:, j, h * 128:(h + 1) * 128], start=True, stop=False),
                                reads=["sT", ("vtok", j)], writes=[pko], inc=False)
                            P.op("pe", C("matmul", bank_o[:, h * 128:(h + 1) * 128], qbd[:, c, h % 2, ts], Sgbf[:, c, :], start=False, stop=True),
                                 reads=["qbd", "Sgbf"], writes=[pko], inc=(h == 3))
                        bank_u, pku = psum()
                        for c in range(2):
                            P.op("pe", C("matmul",
                                bank_u[:, c * 256:(c + 1) * 256], ktok[:, c * 128:(c + 1) * 128], vtok[:, j, c * 256:(c + 1) * 256],
                                start=True, stop=True), reads=["ktok", ("vtok", j)], writes=[pku], inc=(c == 1))
                        lastc = j * 128 + 127
                        for hh in range(2):
                            hr = slice(hh * 64, hh * 64 + 64)
                            buv = bank_u[hr, :].rearrange("q (c h v) -> q c h v", c=2, h=2)[:, :, hh, :]
                            P.op("dve", C("tensor_tensor", out=Sgt[hr, :, :], in0=buv, in1=Sg32[hr, :, :], op=ALU.add),
                                 reads=[pku, "Sg32"], writes=["Sgt"])
                            P.op("dve", C("tensor_tensor", out=Sg32[hr, :, :], in0=Sgt[hr, :, :], in1=bcast(eb[hr, :, lastc], 2, 128), op=ALU.mult),
                                 reads=["Sgt", "lsp"], writes=["Sg32"])
                        P.op("act", C("activation", out=Sgbf[:, :, :], in_=Sg32[:, :, :], func=AF.Copy),
                             reads=["Sg32"], writes=["Sgbf"])
                        P.op("act", C("activation", out=gsq[:, :, :], in_=bank_o[:, :].rearrange("p (h i) -> p h i", h=4),
                                                                          func=AF.Square), reads=[pko], writes=["gsq"])
                        P.op("dve", C("tensor_reduce", out=gss[:, :], in_=gsq[:, :, :], axis=AX.X, op=ALU.add),
                             reads=["gsq"], writes=["gss"])
                        P.op("act", C("activation", out=grs[:, :], in_=gss[:, :], func=AF.Ln, bias=1e-5, scale=1.0 / 128.0),
                             reads=["gss"], writes=["grs"])
                        P.op("act", C("activation", out=grs[:, :], in_=grs[:, :], func=AF.Exp, scale=-0.5),
                             reads=["grs"], writes=["grs"])
                        P.op("dve", C("tensor_tensor", out=gsq[:, :, :], in0=bank_o[:, :].rearrange("p (h i) -> p h i", h=4), in1=bcast(grs[:, :], 2, 128), op=ALU.mult),
                             reads=[pko, "grs", "gsq"], writes=["gsq"])
                        P.op("dve", C("tensor_tensor", out=gon[:, :, :], in0=gsq[:, :, :], in1=bcast(gnw_bc[:, :], 1, 4), op=ALU.mult),
                             reads=["gsq", "gnw_bc"], writes=["gon"])
                        yield
                        bank_t, pkt = psum()
                        bvt = bfv(bank_t)
                        for h in range(4):
                            P.op("pe", C("transpose", bvt[:, h * 128:(h + 1) * 128], gon[:, h, :], ident[:, :]),
                                 reads=["gon", "ident"], writes=[pkt], inc=(h == 3))
                        P.op("dve", C("tensor_tensor",
                            out=oaT[xs][:, :, ts], in0=bvt[:, 0:512].rearrange("p (h i) -> p h i", h=4), in1=siluT[:, :, ts], op=ALU.mult),
                            reads=[pkt, "siluT"], writes=["oaT"])
                    P.dma("sp", "d_scrA", oT_v[:, 0:4, t0:t0 + TB], oaT[xs][:, :, :], reads=["oaT"],
                          writes=[("oTd", blk, 0)])


                    yield
                if part == "rw_front":
                    P.mark("rw_proj")
                    def rw_chunk(cidx, dst_fn, dkeys):
                        hs = cidx % 2
                        o, pk = proj_fm(blk, 1552 + cidx * 128, 128)
                        P.op("act", C("activation", out=hb[hs][:, 1:TB + 1], in_=o, func=AF.Copy),
                             reads=[pk], writes=[("hb", hs)])
                        P.op("pool", C("tensor_copy", hb[hs][:, 0:1], hprev[:, cidx:cidx + 1]),
                             reads=[("hprev", cidx)], writes=[("hb", hs)])
                        P.op("pool", C("tensor_copy", hprev[:, cidx:cidx + 1], hb[hs][:, TB:TB + 1]),
                             reads=[("hb", hs)], writes=[("hprev", cidx)])
                        P.op("dve", C("tensor_tensor", out=dtm[hs][:, :], in0=hb[hs][:, 0:TB], in1=hb[hs][:, 1:TB + 1], op=ALU.subtract),
                             reads=[("hb", hs)], writes=[("dtm", hs)])
                        P.op("dve", C("scalar_tensor_tensor", out=dst_fn(), in0=dtm[hs][:, :], scalar=ppc(PP_MU, cidx),
                                                                     in1=hb[hs][:, 1:TB + 1], op0=ALU.mult, op1=ALU.add),
                             reads=[("dtm", hs), ("hb", hs), "pp"], writes=dkeys)

                    for p in range(4):
                        rw_chunk(p, lambda p=p: r32[:, p, :], [("r32", p)])
                        yield
                    for p in range(4):
                        rw_chunk(4 + p, lambda p=p: k32[:, p, :], [("k32", p)])
                        yield
                    for p in range(4):
                        rw_chunk(8 + p, lambda p=p: vbf[xs][:, p, :], [("vbf", xs, p)])
                        yield
                    rw_chunk(12, lambda: lora32[:, :], ["lora32"])
                    rw_chunk(13, lambda: glow32[:, :], ["glow32"])
                    P.mark("rw_lora")
                    P.op("act", C("activation", out=loraT[0:64, :], in_=lora32[0:64, :], func=AF.Tanh),
                         reads=["lora32"], writes=["loraT"])
                    P.op("act", C("activation", out=loraT[64:128, :], in_=lora32[64:128, :], func=AF.Copy),
                         reads=["lora32"], writes=["loraT"])
                    P.op("act", C("activation", out=sgb[:, :], in_=glow32[:, :], func=AF.Sigmoid),
                         reads=["glow32"], writes=["sgb"])
                    for p in range(4):
                        ps_ = slice(p * 128, (p + 1) * 128)
                        b1, k1 = psum()
                        mm_group(b1[:, 0:TB], [(wupP[:, ps_], loraT[:, :])], k1, reads=["wupP", "loraT"])
                        b2, k2 = psum()
                        mm_group(b2[:, 0:TB], [(aupP[:, ps_], loraT[:, :])], k2, reads=["aupP", "loraT"])
                        b3, k3 = psum()
                        mm_group(b3[:, 0:TB], [(gup[:, ps_], sgb[:, :])], k3, reads=["gup", "sgb"])
                        P.op("dve", C("tensor_scalar", tmp["kkr"][:, :], k32[:, p, :], ppc(PP_KK, p), None, ALU.mult),
                             reads=[("k32", p), "pp"], writes=["kkr"])
                        P.op("act", C("activation", out=tmp["sig"][:, :], in_=b1[:, 0:TB], func=AF.Sigmoid, bias=ppc(PP_W0, p), scale=1.0),
                             reads=[k1, "pp"], writes=["sig"])
                        P.op("act", C("activation", out=tmp["a32"][:, :], in_=b2[:, 0:TB], func=AF.Sigmoid, bias=ppc(PP_A0, p), scale=1.0),
                             reads=[k2, "pp"], writes=["a32"])
                        P.op("act", C("activation", out=sqb[:, :], in_=tmp["kkr"][:, :], func=AF.Square), reads=["kkr"], writes=["sqb"])
                        P.op("act", C("activation", out=gT[xs][:, p, :], in_=b3[:, 0:TB], func=AF.Copy), reads=[k3], writes=[("gT", xs, p)])
                        yield
                        b4, k4 = psum()
                        mm_group(b4[:, 0:TB], [(ones_bd[:, :], sqb[:, :])], k4, reads=["ones_bd", "sqb"])
                        P.op("dve", C("tensor_tensor_scan", tmp["cs"][:, :], rm64[:, :], tmp["sig"][:, :], 0.0, ALU.mult, ALU.add),
                             reads=["sig", "rm64"], writes=["cs"])
                        P.op("dve", C("tensor_scalar", tmp["tt"][:, :], tmp["a32"][:, :], ppc(PP_KA, p), dpar[:, 2 + p:3 + p], ALU.mult, ALU.add),
                             reads=["a32", "pp", "dpar"], writes=["tt"])
                        P.op("dve", C("tensor_tensor", out=tmp["csx"][:, :], in0=tmp["cs"][:, :], in1=tmp["sig"][:, :], op=ALU.subtract),
                             reads=["cs", "sig"], writes=["csx"])
                        P.op("dve", C("tensor_tensor", out=tmp["kp"][:, :], in0=k32[:, p, :], in1=tmp["tt"][:, :], op=ALU.mult),
                             reads=[("k32", p), "tt"], writes=["kp"])
                        P.op("act", C("activation", out=tmp["lnss"][:, :], in_=b4[:, 0:TB], func=AF.Ln, bias=1e-24, scale=1.0),
                             reads=[k4], writes=["lnss"])
                        P.op("act", C("activation", out=tmp["En"][:, :], in_=tmp["cs"][:, :], func=AF.Exp, scale=-C0), reads=["cs"], writes=["En"])
                        P.op("act", C("activation", out=tmp["inv"][:, :], in_=tmp["lnss"][:, :], func=AF.Exp, scale=-0.5), reads=["lnss"], writes=["inv"])
                        P.op("act", C("activation", out=tmp["Enx"][:, :], in_=tmp["csx"][:, :], func=AF.Exp, scale=-C0), reads=["csx"], writes=["Enx"])
                        P.op("act", C("activation", out=tmp["Ep"][:, :], in_=tmp["cs"][:, :], func=AF.Exp, scale=C0), reads=["cs"], writes=["Ep"])
                        P.op("dve", C("scalar_tensor_tensor", out=rkb[:, :], in0=r32[:, p, :], scalar=ppc(PP_RK, p), in1=tmp["kp"][:, :],
                                      op0=ALU.mult, op1=ALU.mult), reads=[("r32", p), "pp", "kp"], writes=["rkb"])
                        yield
                        b5, k5 = psum()
                        mm_group(b5[:, 0:TB], [(ones_bd[:, :], rkb[:, :])], k5, reads=["ones_bd", "rkb"])
                        P.op("dve", C("tensor_tensor", out=tmp["kk"][:, :], in0=tmp["kkr"][:, :], in1=tmp["inv"][:, :], op=ALU.mult),
                             reads=["kkr", "inv"], writes=["kk"])
                        P.op("dve", C("tensor_copy", wcl[xs][:, p, :], tmp["En"][:, 63:TB:64]), reads=["En"], writes=[("wcl", xs, p)])
                        P.op("dve", C("scalar_tensor_tensor", out=ARc[xs][:, p, 0, :], in0=tmp["kk"][:, :], scalar=-1.0, in1=tmp["Enx"][:, :],
                                      op0=ALU.mult, op1=ALU.mult), reads=["kk", "Enx"], writes=[("ARc", xs, p)])
                        P.op("act", C("activation", out=coef[xs][:, p, :], in_=b5[:, 0:TB], func=AF.Copy), reads=[k5], writes=[("coef", xs, p)])
                        P.op("dve", C("tensor_tensor", out=tmp["csx"][:, :], in0=tmp["kk"][:, :], in1=tmp["a32"][:, :], op=ALU.mult),
                             reads=["kk", "a32"], writes=["csx"])
                        P.op("dve", C("tensor_tensor", out=BKc[xs][:, p, 1, :], in0=tmp["kp"][:, :], in1=tmp["Ep"][:, :], op=ALU.mult),
                             reads=["kp", "Ep"], writes=[("BKc", xs, p)])
                        P.op("dve", C("tensor_tensor", out=BKc[xs][:, p, 0, :], in0=tmp["csx"][:, :], in1=tmp["Ep"][:, :], op=ALU.mult),
                             reads=["csx", "Ep"], writes=[("BKc", xs, p)])
                        P.op("dve", C("tensor_tensor", out=ARc[xs][:, p, 1, :], in0=r32[:, p, :], in1=tmp["En"][:, :], op=ALU.mult),
                             reads=[("r32", p), "En"], writes=[("ARc", xs, p)])
                        yield

                    yield
                if part == "chunks":
                    allp = lambda n: [(n, xs, p) for p in range(4)] if n in ("ARc", "BKc", "vbf", "wcl", "coef", "gT") else [(n, p) for p in range(4)]
                    P.mark("rw_chunks")

                    def v4(bank_, half):
                        return bank_[:, :].rearrange("q (p a t) -> q p a t", p=2, a=2)[:, :, half, :]

                    def v3(bank_):
                        return bank_[:, :].rearrange("q (p t) -> q p t", p=4)

                    def prep(c):
                        cs_ = slice(c * 64, (c + 1) * 64)
                        bs = c % 2
                        AR, BK, VT = ARbd[bs], BKbd[bs], VTbd[bs]
                        kAR, kBK, kVT = ("ARbd", bs), ("BKbd", bs), ("VTbd", bs)
                        for hh in range(2):
                            hr = slice(hh * 64, hh * 64 + 64)
                            if hh == 0:
                                P.op("act", C("activation", out=AR[hr, :, :, hr], in_=ARc[xs][hr, :, :, cs_], func=AF.Copy), reads=allp("ARc"), writes=[kAR])
                            else:
                                P.op("dve", C("tensor_copy", AR[hr, :, :, hr], ARc[xs][hr, :, :, cs_]), reads=allp("ARc"), writes=[kAR])
                            pass
                            if hh == 1:
                                P.op("act", C("activation", out=BK[hr, :, :, hr], in_=BKc[xs][hr, :, :, cs_], func=AF.Copy), reads=allp("BKc"), writes=[kBK])
                            else:
                                P.op("dve", C("tensor_copy", BK[hr, :, :, hr], BKc[xs][hr, :, :, cs_]), reads=allp("BKc"), writes=[kBK])
                            pass
                            P.op("pool", C("tensor_copy", VT[hr, :, hr], vbf[xs][hr, :, cs_]), reads=allp("vbf"), writes=[kVT])
                        yield
                        bQ0, kQ0 = psum()
                        bRb, kRb = psum()
                        b3, k3 = psum()
                        for p in range(4):
                            P.op("pe", C("matmul", bQ0[:, p * 128:(p + 1) * 128], BK[:, p, 0, :], AR[:, p, 0, :], start=True, stop=True),
                                 reads=[kBK, kAR], writes=[kQ0], inc=(p == 3))
                        for p in range(4):
                            P.op("pe", C("matmul", b3[:, p * 128:(p + 1) * 128], AR[:, p, 0, :], BK[:, p, 0, :], start=True, stop=True),
                                 reads=[kBK, kAR], writes=[k3], inc=(p == 3))
                        for p in range(4):
                            P.op("pe", C("matmul", bRb[:, p * 128:(p + 1) * 128], BK[:, p, 0, :], AR[:, p, 1, :], start=True, stop=True),
                                 reads=[kBK, kAR], writes=[kRb], inc=(p == 3))
                        qs = 0
                        ps0 = 0
                        QXs, Pts = QX[bs], Pt[bs]
                        kQX = lambda i: ("QX", bs, i)
                        kPt = lambda i: ("Pt", bs, i)
                        P.op("dve", C("tensor_tensor", out=QXs[qs][:, :, 0, :], in0=v3(bQ0), in1=bcast(m_su[:, :], 1, 4), op=ALU.mult),
                             reads=[kQ0, "m_su"], writes=[kQX(qs)])
                        P.op("dve", C("tensor_tensor", out=Pts[ps0][:, :, :], in0=v3(b3), in1=bcast(m_sl[:, :], 1, 4), op=ALU.mult),
                             reads=[k3, "m_sl"], writes=[kPt(ps0)])
                        P.op("dve", C("tensor_tensor", out=QXs[1 - qs][:, :, 1, :], in0=QXs[qs][:, :, 0, :], in1=bcast(identf[:, :], 1, 4), op=ALU.add),
                             reads=[kQX(qs), "identf"], writes=[kQX(1 - qs)])
                        P.op("dve", C("tensor_tensor", out=Arb[bs][:, :, :], in0=v3(bRb), in1=bcast(m_iu[:, :], 1, 4), op=ALU.mult),
                             reads=[kRb, "m_iu"], writes=[("Arb", bs)])
                        yield
                        bAk, kAk = psum()
                        bRk, kRk = psum()
                        for p in range(4):
                            P.op("pe", C("matmul", bAk[:, p * 128:(p + 1) * 128], BK[:, p, 1, :], AR[:, p, 0, :], start=True, stop=True),
                                 reads=[kBK, kAR], writes=[kAk], inc=(p == 3))
                        for p in range(4):
                            P.op("pe", C("matmul", bRk[:, p * 128:(p + 1) * 128], BK[:, p, 1, :], AR[:, p, 1, :], start=True, stop=True),
                                 reads=[kBK, kAR], writes=[kRk], inc=(p == 3))
                        P.op("dve", C("tensor_tensor", out=Aak[bs][:, :, :], in0=v3(bAk), in1=bcast(m_su[:, :], 1, 4), op=ALU.mult),
                             reads=[kAk, "m_su"], writes=[("Aak", bs)])
                        P.op("dve", C("tensor_tensor", out=Ark[bs][:, :, :], in0=v3(bRk), in1=bcast(m_iu[:, :], 1, 4), op=ALU.mult),
                             reads=[kRk, "m_iu"], writes=[("Ark", bs)])
                        yield
                        for lvl in range(6):
                            Pn, Qn = Pts[ps0], QXs[qs]
                            kP, kQ = kPt(ps0), kQX(qs)
                            Pnew, Qnew = Pts[1 - ps0], QXs[1 - qs]
                            kPn, kQn = kPt(1 - ps0), kQX(1 - qs)
                            if lvl <= 4:
                                bp, kp_ = psum()
                                for p in range(4):
                                    P.op("pe", C("matmul", bp[:, p * 128:(p + 1) * 128], Qn[:, p, 0, :], Pn[:, p, :], start=True, stop=True),
                                         reads=[kP, kQ], writes=[kp_], inc=(p == 3))
                            if lvl == 0:
                                bq, kq_ = psum()
                                for p in range(4):
                                    P.op("pe", C("matmul", bq[:, p * 128:(p + 1) * 128], Pn[:, p, :], Qn[:, p, 0, :], start=True, stop=True),
                                         reads=[kP, kQ], writes=[kq_], inc=(p == 3))
                                P.op("dve", C("tensor_copy", Qnew[:, :, 0, :], v3(bq)), reads=[kq_], writes=[kQn])
                                pass
                            elif lvl <= 3:
                                bq, kq_ = psum()
                                bx, kx_ = psum()
                                for p in range(4):
                                    P.op("pe", C("matmul", bq[:, p * 128:(p + 1) * 128], Pn[:, p, :], Qn[:, p, 0, :], start=True, stop=True),
                                         reads=[kP, kQ], writes=[kq_], inc=(p == 3))
                                for p in range(4):
                                    P.op("pe", C("matmul", bx[:, p * 128:(p + 1) * 128], Pn[:, p, :], Qn[:, p, 1, :], start=True, stop=True),
                                         reads=[kP, kQ], writes=[kx_], inc=(p == 3))
                                P.op("act", C("activation", out=Qnew[:, :, 0, :], in_=v3(bq), func=AF.Copy), reads=[kq_], writes=[kQn])
                                P.op("dve", C("tensor_tensor", out=Qnew[:, :, 1, :], in0=v3(bx), in1=Qn[:, :, 1, :], op=ALU.add),
                                     reads=[kx_, kQ], writes=[kQn])
                            else:
                                bq, kq_ = psum()
                                for p in range(4):
                                    P.op("pe", C("matmul", bq[:, p * 128:(p + 1) * 128], Pn[:, p, :], Qn[:, p, 1, :], start=True, stop=True),
                                         reads=[kP, kQ], writes=[kq_], inc=(p == 3))
                                P.op("dve", C("tensor_tensor", out=Qnew[:, :, 1, :], in0=v3(bq), in1=Qn[:, :, 1, :], op=ALU.add),
                                     reads=[kq_, kQ], writes=[kQn])
                            if lvl <= 4:
                                P.op("act", C("activation", out=Pnew[:, :, :], in_=v3(bp), func=AF.Copy), reads=[kp_], writes=[kPn])
                            ps0, qs = 1 - ps0, 1 - qs
                            yield
                        assert qs == 0
                        for (src_fn, dst, dkey, skey, eng_) in ((lambda p: BK[:, p, 0, :], Btok[bs], ("Btok", bs), kBK, "act"),
                                                                (lambda p: BK[:, p, 1, :], Ktok[bs], ("Ktok", bs), kBK, "dve"),
                                                                (lambda p: VT[:, p, :], Vtok[bs], ("Vtok", bs), kVT, "act")):
                            bt, kt = psum()
                            btv = bfv(bt)
                            for p in range(4):
                                P.op("pe", C("transpose", btv[:, p * 128:(p + 1) * 128], src_fn(p), ident[:, :]),
                                     reads=[skey, "ident"], writes=[kt], inc=(p == 3))
                            srcv = btv[:, 0:512].rearrange("q (p t) -> q p t", p=4)
                            if eng_ == "act":
                                P.op("act", C("activation", out=dst[:, :, :], in_=srcv, func=AF.Copy), reads=[kt], writes=[dkey])
                            else:
                                P.op("dve", C("tensor_copy", dst[:, :, :], srcv), reads=[kt], writes=[dkey])
                            yield

                    def post(c):
                        cs_ = slice(c * 64, (c + 1) * 64)
                        bs = c % 2
                        AR = ARbd[bs]
                        kAR = ("ARbd", bs)
                        XT, kXT = QX[bs][0], ("QX", bs, 0)
                        br, kr = psum()
                        for p in range(4):
                            P.op("pe", C("matmul", br[:, p * 128:(p + 1) * 128], AR[:, p, 0, :], Sbf[:, p, :], start=True, stop=False),
                                 reads=[kAR, "Sbf"], writes=[kr], inc=False)
                            P.op("pe", C("matmul", br[:, p * 128:(p + 1) * 128], Aak[bs][:, p, :], Vtok[bs][:, p, :], start=False, stop=True),
                                 reads=[("Aak", bs), ("Vtok", bs)], writes=[kr], inc=(p == 3))
                        P.op("act", C("activation", out=RH0[:, :, :], in_=v3(br), func=AF.Copy), reads=[kr], writes=["RH0"])
                        yield
                        bu, ku = psum()
                        for p in range(4):
                            P.op("pe", C("matmul", bu[:, p * 128:(p + 1) * 128], XT[:, p, 1, :], RH0[:, p, :], start=True, stop=True),
                                 reads=[kXT, "RH0"], writes=[ku], inc=(p == 3))
                        P.op("act", C("activation", out=Ubf[:, :, :], in_=v3(bu), func=AF.Copy), reads=[ku], writes=["Ubf"])
                        yield
                        by, ky = psum()
                        for p in range(4):
                            P.op("pe", C("matmul", by[:, p * 128:(p + 1) * 128], AR[:, p, 1, :], Sbf[:, p, :], start=True, stop=False),
                                 reads=[kAR, "Sbf"], writes=[ky], inc=False)
                            P.op("pe", C("matmul", by[:, p * 128:(p + 1) * 128], Arb[bs][:, p, :], Ubf[:, p, :], start=False, stop=False),
                                 reads=[("Arb", bs), "Ubf"], writes=[ky], inc=False)
                            P.op("pe", C("matmul", by[:, p * 128:(p + 1) * 128], Ark[bs][:, p, :], Vtok[bs][:, p, :], start=False, stop=True),
                                 reads=[("Ark", bs), ("Vtok", bs)], writes=[ky], inc=(p == 3))
                        bs_, ks_ = psum()
                        for p in range(4):
                            P.op("pe", C("matmul", bs_[:, p * 128:(p + 1) * 128], Btok[bs][:, p, :], Ubf[:, p, :], start=True, stop=False),
                                 reads=[("Btok", bs), "Ubf"], writes=[ks_], inc=False)
                            P.op("pe", C("matmul", bs_[:, p * 128:(p + 1) * 128], Ktok[bs][:, p, :], Vtok[bs][:, p, :], start=False, stop=True),
                                 reads=[("Ktok", bs), ("Vtok", bs)], writes=[ks_], inc=(p == 3))
                        P.op("dve", C("tensor_tensor", out=St[:, :, :], in0=v3(bs_), in1=S32[:, :, :], op=ALU.add),
                             reads=[ks_, "S32"], writes=["St"])
                        P.op("dve", C("tensor_tensor", out=S32[:, :, :], in0=St[:, :, :], in1=bcast(wcl[xs][:, :, c], 2, 128), op=ALU.mult),
                             reads=["St"] + allp("wcl"), writes=["S32"])
                        P.op("dve", C("tensor_copy", Sbf[:, :, :], S32[:, :, :]), reads=["S32"], writes=["Sbf"])
                        P.op("act", C("activation", out=yraw[:, :, :], in_=v3(by), func=AF.Copy), reads=[ky], writes=["yraw"])
                        yield
                        bt, kt = psum()
                        btv = bfv(bt)
                        for p in range(4):
                            P.op("pe", C("transpose", btv[:, p * 128:(p + 1) * 128], yraw[:, p, :], ident[:, :]),
                                 reads=["yraw", "ident"], writes=[kt], inc=(p == 3))
                        for hh in range(2):
                            hr = slice(hh * 64, hh * 64 + 64)
                            P.op("act", C("activation", out=ynT[xs][hr, :, cs_], in_=btv[hr, 0:512].rearrange("q (p t) -> q p t", p=4)[:, :, hr],
                                          func=AF.Copy), reads=[kt], writes=[("ynT", xs, p) for p in range(4)])
                        yield

                    NCK = TB // 64
                    for c0 in range(0, NCK, 2):
                        yield from ileave(prep(c0), prep(c0 + 1))
                        yield from post(c0)
                        yield from post(c0 + 1)
                if part == "rw_out":
                    allp = lambda n: [(n, xs, p) for p in range(4)]
                    P.mark("rw_out")
                    for p in range(4):
                        b1_, k1_ = psum()
                        mm_group(b1_[:, 0:TB], [(ones_bd[:, :], ynT[xs][:, p, :])], k1_, reads=["ones_bd", ("ynT", xs, p)])
                        P.op("act", C("activation", out=ysqb[:, :], in_=ynT[xs][:, p, :], func=AF.Square), reads=[("ynT", xs, p)], writes=["ysqb"])
                        b2_, k2_ = psum()
                        mm_group(b2_[:, 0:TB], [(ones_bd[:, :], ysqb[:, :])], k2_, reads=["ones_bd", "ysqb"])
                        P.op("act", C("activation", out=ym[:, :], in_=b1_[:, 0:TB], func=AF.Copy, scale=1.0 / 64.0), reads=[k1_], writes=["ym"])
                        P.op("dve", C("tensor_tensor", out=yv[:, :], in0=ym[:, :], in1=ym[:, :], op=ALU.mult), reads=["ym"], writes=["yv"])
                        P.op("dve", C("scalar_tensor_tensor", out=yv[:, :], in0=b2_[:, 0:TB], scalar=1.0 / 64.0, in1=yv[:, :], op0=ALU.mult, op1=ALU.subtract),
                             reads=[k2_, "yv"], writes=["yv"])
                        P.op("act", C("activation", out=yv[:, :], in_=yv[:, :], func=AF.Ln, bias=64e-5, scale=1.0), reads=["yv"], writes=["yv"])
                        P.op("act", C("activation", out=yv[:, :], in_=yv[:, :], func=AF.Exp, scale=-0.5), reads=["yv"], writes=["yv"])
                        P.op("dve", C("tensor_tensor", out=ot1[:, :], in0=ynT[xs][:, p, :], in1=ym[:, :], op=ALU.subtract), reads=[("ynT", xs, p), "ym"], writes=["ot1"])
                        P.op("dve", C("tensor_tensor", out=ot1[:, :], in0=ot1[:, :], in1=yv[:, :], op=ALU.mult), reads=["ot1", "yv"], writes=["ot1"])
                        P.op("dve", C("tensor_tensor", out=ot2[:, :], in0=coef[xs][:, p, :], in1=vbf[xs][:, p, :], op=ALU.mult), reads=[("coef", xs, p), ("vbf", xs, p)], writes=["ot2"])
                        P.op("dve", C("tensor_scalar", ot1[:, :], ot1[:, :], ppc(PP_GNW, p), ppc(PP_GNB, p), ALU.mult, ALU.add),
                             reads=["ot1", "pp"], writes=["ot1"])
                        P.op("dve", C("tensor_tensor", out=ot1[:, :], in0=ot1[:, :], in1=ot2[:, :], op=ALU.add),
                             reads=["ot1", "ot2"], writes=["ot1"])
                        P.op("dve", C("tensor_tensor", out=obT[xs][:, p, :], in0=ot1[:, :], in1=gT[xs][:, p, :], op=ALU.mult), reads=["ot1", ("gT", xs, p)], writes=["obT"])
                        yield
                    P.dma("sp", "d_scrB", oT_v[:, 4:8, t0:t0 + TB], obT[xs][:, :, :], reads=["obT"],
                          writes=[("oTd", blk, 1)])

            def run(g):
                for _ in g:
                    pass

            def ileave(*gens, weights=None):
                gens = list(gens)
                w = {id(g): (weights[i] if weights else 1) for i, g in enumerate(gens)}
                while gens:
                    for g in list(gens):
                        for _ in range(w[id(g)]):
                            try:
                                next(g)
                                yield
                            except StopIteration:
                                gens.remove(g)
                                break

            run(emit_block(0, 'front_gla'))
            for blk in range(NBLK):
                if blk + 1 < NBLK:
                    P.dma("pool", "d_xT%d" % ((blk + 1) % 2), xTb[(blk + 1) % 2][:, :, :],
                          xT_v[:, :, (blk + 1) * TB:(blk + 2) * TB], writes=[("xTb", (blk + 1) % 2)])
                if blk == 0:
                    run(emit_block(0, 'rw_front'))

                def side(blk=blk):
                    if blk > 0:
                        yield from emit_block(blk - 1, 'rw_out')
                    yield from emit_block(blk, 'gla_core')
                    if blk + 1 < NBLK:
                        yield from emit_block(blk + 1, 'rw_front')
                        yield from emit_block(blk + 1, 'front_gla')
                run(ileave(emit_block(blk, 'chunks'), side(), weights=SW))
            run(emit_block(NBLK - 1, 'rw_out'))


        def layer_norm(eng2, src, dst, gbc, bbc, stt, mvt, keys):
            ksrc, kdst = keys
            for hh in range(2):
                P.op("dve", C("bn_stats", stt[:, hh, :], src[:, hh * 512:(hh + 1) * 512]), reads=[ksrc], writes=["ln_st"])
            P.op("dve", C("bn_aggr", mvt[:, 0:2], stt[:, :, :].rearrange("p a b -> p (a b)")), reads=["ln_st"], writes=["ln_mv"])
            P.op("act", C("activation", out=mvt[:, 2:3], in_=mvt[:, 1:2], func=AF.Ln, bias=1e-5, scale=1.0), reads=["ln_mv"], writes=["ln_mv"])
            P.op("act", C("activation", out=mvt[:, 2:3], in_=mvt[:, 2:3], func=AF.Exp, scale=-0.5), reads=["ln_mv"], writes=["ln_mv"])
            P.op("dve", C("tensor_scalar", src[:, :], src[:, :], mvt[:, 0:1], mvt[:, 2:3], ALU.subtract, ALU.mult), reads=[ksrc, "ln_mv"], writes=[ksrc])
            P.op(eng2, C("tensor_tensor", out=src[:, :], in0=src[:, :], in1=gbc[:, :], op=ALU.mult), reads=[ksrc, "lnp"], writes=[ksrc])
            P.op(eng2, C("tensor_tensor", out=dst[:, :], in0=src[:, :], in1=bbc[:, :], op=ALU.add), reads=[ksrc, "lnp"], writes=[kdst])

        if phases >= 2:
          P.barrier()
          with contextlib.ExitStack() as sbc:
            w1sb = sbt(sbc, "w1sb", [128, KC, DFF], BF16)
            lnst = sbt(sbc, "lnst", [128, 2, 6], F32)
            lnmv = sbt(sbc, "lnmv", [128, 4], F32)
            w1_v = w1_d.rearrange("(kc p) n -> p kc n", p=128)
            with contextlib.ExitStack() as sb_:
                wmg = sbt(sb_, "wmg", [128, KC, 2 * D], BF16)
                wbr = sbt(sb_, "wbr", [128, KC, D], BF16)
                wout = sbt(sb_, "wout", [128, KC, D], BF16)
                g1bc = sbt(sb_, "g1bc", [128, D], F32)
                b1bc = sbt(sb_, "b1bc", [128, D], F32)
                xTb2 = [sbt(sb_, "xTb2_%d" % i, [128, KC, TB], BF16) for i in range(2)]
                oTb = [sbt(sb_, "oTb%d" % i, [128, KC, TB], BF16) for i in range(2)]
                gate = sbt(sb_, "gate", [128, 16, TB], BF16)
                m32 = sbt(sb_, "m32", [128, TB], F32)
                tg = sbt(sb_, "tg", [128, TB], F32)
                mT = sbt(sb_, "mT", [128, KC, TB], BF16)
                xtok = [sbt(sb_, "xtok%d" % i, [128, D], F32) for i in range(4)]
                rsd = [sbt(sb_, "rsd%d" % i, [128, D], F32) for i in range(2)]
                w_mg_v = w_mg_d.rearrange("(kc p) n -> p kc n", p=128)
                w_br_v = w_br_d.rearrange("(kc p) n -> p kc n", p=128)
                w_out_v = w_out_d.rearrange("(kc p) n -> p kc n", p=128)
                P.dma("pool", "d_xB0", xTb2[0][:, :, :], xT_v[:, :, 0:TB], writes=[("xTb2", 0)])
                for g in range(4):
                    P.dma("pool", "d_wmg%d" % g, wmg[:, :, g * 512:(g + 1) * 512], w_mg_v[:, :, g * 512:(g + 1) * 512], writes=[("wmg", g)])
                for g in range(2):
                    P.dma("pool", "d_wbr%d" % g, wbr[:, :, g * 512:(g + 1) * 512], w_br_v[:, :, g * 512:(g + 1) * 512], writes=[("wbr", g)])
                for g in range(2):
                    P.dma("pool", "d_wout%d" % g, wout[:, :, g * 512:(g + 1) * 512], w_out_v[:, :, g * 512:(g + 1) * 512], writes=[("wout", g)])
                P.dma("sp", "d_ln1g", g1bc[:, :], rows_d[0:1, R_LN1G:R_LN1G + D].to_broadcast([128, D]), writes=["lnp"])
                P.dma("sp", "d_ln1b", b1bc[:, :], rows_d[0:1, R_LN1B:R_LN1B + D].to_broadcast([128, D]), writes=["lnp"])
                w1_issued = False

                def b_loads(bb):
                    P.dma("sp", "d_oTb%d" % (bb % 2), oTb[bb % 2][:, :, :], oT_v[:, :, bb * TB:(bb + 1) * TB],
                          reads=[("oTd", bb, 0), ("oTd", bb, 1)], writes=[("oTb", bb % 2)])
                    for jj in range(TB // 128):
                        tt_ = bb * (TB // 128) + jj
                        P.dma("sp", "d_xtok%d" % (tt_ % 4), xtok[tt_ % 4][:, :], x_d[tt_ * 128:(tt_ + 1) * 128, :], writes=[("xtok", tt_ % 4)])

                for blk in range(NBLK):
                    xs = blk % 2
                    t0 = blk * TB
                    if blk + 1 < NBLK:
                        P.dma("pool", "d_xB%d" % ((blk + 1) % 2), xTb2[(blk + 1) % 2][:, :, :], xT_v[:, :, t0 + TB:t0 + 2 * TB],
                              writes=[("xTb2", (blk + 1) % 2)])
                    if not w1_issued:
                        for g in range(8):
                            P.dma("pool", "d_w1_%d" % g, w1sb[:, :, g * 512:(g + 1) * 512], w1_v[:, :, g * 512:(g + 1) * 512], writes=[("w1", g)])
                        w1_issued = True
                    if blk == 0:
                        b_loads(0)
                    if blk + 1 < NBLK:
                        b_loads(blk + 1)
                    for c in range(16):
                        bank, pk = psum()
                        mm_group(bank[:, 0:TB], [(wmg[:, kc, c * 128:(c + 1) * 128], xTb2[xs][:, kc, :]) for kc in range(KC)], pk,
                                 reads=[("wmg", c // 4), ("xTb2", xs)])
                        P.op("act", C("activation", out=gate[:, c, :], in_=bank[:, 0:TB], func=AF.Sigmoid, bias=ppc(PP_BM, c), scale=1.0),
                             reads=[pk, "pp"], writes=[("gate", c)])
                    for c in range(8):
                        ba, ka = psum()
                        mm_group(ba[:, 0:TB], [(wbr[:, kc, c * 128:(c + 1) * 128], oTb[xs][:, kc, :]) for kc in range(0, 4)], ka,
                                 reads=[("wbr", c // 4), ("oTb", xs)])
                        bb, kb = psum()
                        mm_group(bb[:, 0:TB], [(wbr[:, kc, c * 128:(c + 1) * 128], oTb[xs][:, kc, :]) for kc in range(4, 8)], kb,
                                 reads=[("wbr", c // 4), ("oTb", xs)])
                        P.op("dve", C("tensor_tensor", out=m32[:, :], in0=ba[:, 0:TB], in1=gate[:, c, :], op=ALU.mult), reads=[ka, ("gate", c)], writes=["m32"])
                        P.op("dve", C("tensor_tensor", out=tg[:, :], in0=bb[:, 0:TB], in1=gate[:, 8 + c, :], op=ALU.mult), reads=[kb, ("gate", 8 + c)], writes=["tg"])
                        P.op("dve", C("tensor_tensor", out=mT[:, c, :], in0=m32[:, :], in1=tg[:, :], op=ALU.add), reads=["m32", "tg"], writes=[("mT", c)])
                    for j in range(TB // 128):
                        ti = blk * (TB // 128) + j
                        sl = ti % 2
                        for hh in range(2):
                            bank, pk = psum()
                            mm_group(bank[:, :], [(mT[:, kc, j * 128:(j + 1) * 128], wout[:, kc, hh * 512:(hh + 1) * 512]) for kc in range(KC)], pk,
                                     reads=[("wout", hh)] + [("mT", c) for c in range(8)])
                            P.op("dve", C("scalar_tensor_tensor", out=rsd[sl][:, hh * 512:(hh + 1) * 512], in0=xtok[ti % 4][:, hh * 512:(hh + 1) * 512],
                                          scalar=ALPHA, in1=bank[:, :], op0=ALU.mult, op1=ALU.add), reads=[pk, ("xtok", ti % 4)], writes=[("rsd", sl)])
                        layer_norm("pool", rsd[sl], rsd[sl], g1bc, b1bc, lnst, lnmv, (("rsd", sl), ("rsd", sl)))
                        P.dma("sp", "d_scrX%d" % sl, x1_d[ti * 128:(ti + 1) * 128, :], rsd[sl][:, :], reads=[("rsd", sl)], writes=[("x1d", ti)])

            if phases >= 3:
              P.barrier()
              with contextlib.ExitStack() as sc_:
                w2sb = sbt(sc_, "w2sb", [128, 32, D], BF16)
                g2bc = sbt(sc_, "g2bc", [128, D], F32)
                b2lbc = sbt(sc_, "b2lbc", [128, D], F32)
                bdbc = sbt(sc_, "bdbc", [128, D], F32)
                x1tok = [sbt(sc_, "x1tok%d" % i, [128, D], F32) for i in range(4)]
                x1bf = [sbt(sc_, "x1bf%d" % i, [128, D], BF16) for i in range(2)]
                x1T = sbt(sc_, "x1T", [128, KC, TB], BF16)
                rl = [sbt(sc_, "rl%d" % i, [128, TB], F32) for i in range(2)]
                hT = sbt(sc_, "hT", [128, 32, TB], BF16)
                rs2 = [sbt(sc_, "rs2_%d" % i, [128, D], F32) for i in range(2)]
                w2_v = w2_d.rearrange("(f p) n -> p f n", p=128)
                for g in range(4):
                    P.dma("pool", "d_w2_%d" % g, w2sb[:, g * 8:(g + 1) * 8, :], w2_v[:, g * 8:(g + 1) * 8, :], writes=[("w2", g)])
                P.dma("sp", "d_ln2g", g2bc[:, :], rows_d[0:1, R_LN2G:R_LN2G + D].to_broadcast([128, D]), writes=["lnp"])
                P.dma("sp", "d_ln2b", b2lbc[:, :], rows_d[0:1, R_LN2B:R_LN2B + D].to_broadcast([128, D]), writes=["lnp"])
                P.dma("sp", "d_bd", bdbc[:, :], rows_d[0:1, R_B2:R_B2 + D].to_broadcast([128, D]), writes=["bdbc"])
                def c_loads(bb):
                    for jj in range(TB // 128):
                        tt_ = bb * (TB // 128) + jj
                        P.dma("sp", "d_x1t%d" % (tt_ % 4), x1tok[tt_ % 4][:, :], x1_d[tt_ * 128:(tt_ + 1) * 128, :], reads=[("x1d", tt_)],
                              writes=[("x1tok", tt_ % 4)])

                def c_tr(bb):
                    for jj in range(TB // 128):
                        tt_ = bb * (TB // 128) + jj
                        P.op("act", C("activation", out=x1bf[jj][:, :], in_=x1tok[tt_ % 4][:, :], func=AF.Copy), reads=[("x1tok", tt_ % 4)], writes=[("x1bf", jj)])
                        bt, kt = psum()
                        btv = bfv(bt)
                        for kc in range(KC):
                            P.op("pe", C("transpose", btv[:, kc * 128:(kc + 1) * 128], x1bf[jj][:, kc * 128:(kc + 1) * 128], ident[:, :]),
                                 reads=[("x1bf", jj), "ident"], writes=[kt], inc=(kc == KC - 1))
                        P.op("dve", C("tensor_copy", x1T[:, :, jj * 128:(jj + 1) * 128], btv[:, :].rearrange("p (kc t) -> p kc t", kc=KC)),
                             reads=[kt], writes=[("x1T", jj)])

                c_loads(0)
                c_tr(0)
                for blk in range(NBLK):
                    if blk + 1 < NBLK:
                        c_loads(blk + 1)
                    for f in range(32):
                        bank, pk = psum()
                        mm_group(bank[:, 0:TB], [(w1sb[:, kc, f * 128:(f + 1) * 128], x1T[:, kc, :]) for kc in range(KC)], pk,
                                 reads=[("w1", f // 4)] + [("x1T", j) for j in range(TB // 128)])
                        rs = f % 2
                        P.op("act", C("activation", out=rl[rs][:, :], in_=bank[:, 0:TB], func=AF.Relu, bias=ppc(PP_B1, f), scale=1.0),
                             reads=[pk, "pp"], writes=[("rl", rs)])
                        P.op("dve", C("tensor_tensor", out=hT[:, f, :], in0=rl[rs][:, :], in1=rl[rs][:, :], op=ALU.mult), reads=[("rl", rs)], writes=[("hT", f)])
                    if blk + 1 < NBLK:
                        c_tr(blk + 1)
                    for j in range(TB // 128):
                        ti = blk * (TB // 128) + j
                        sl = ti % 2
                        for hh in range(2):
                            bank, pk = psum()
                            mm_group(bank[:, :], [(hT[:, f, j * 128:(j + 1) * 128], w2sb[:, f, hh * 512:(hh + 1) * 512]) for f in range(32)], pk,
                                     reads=[("w2", g) for g in range(4)] + [("hT", f) for f in range(32)])
                            P.op("dve", C("scalar_tensor_tensor", out=rs2[sl][:, hh * 512:(hh + 1) * 512], in0=x1tok[ti % 4][:, hh * 512:(hh + 1) * 512],
                                          scalar=ALPHA, in1=bank[:, :], op0=ALU.mult, op1=ALU.add), reads=[pk, ("x1tok", ti % 4)], writes=[("rs2", sl)])
                        P.op("pool", C("tensor_tensor", out=rs2[sl][:, :], in0=rs2[sl][:, :], in1=bdbc[:, :], op=ALU.add), reads=[("rs2", sl), "bdbc"], writes=[("rs2", sl)])
                        layer_norm("pool", rs2[sl], rs2[sl], g2bc, b2lbc, lnst, lnmv, (("rs2", sl), ("rs2", sl)))
                        P.dma("sp", "d_out%d" % sl, out_d[ti * 128:(ti + 1) * 128, :], rs2[sl][:, :], reads=[("rs2", sl)], writes=[("outd", ti)])

        P.final_wait("sp", [k for k in P.dcnt if k.startswith("d_out") or k.startswith("d_scr")])
        P.emit()
    return nc


def prep_shared(inp):
    f = lambda a: np.ascontiguousarray(np.asarray(a, dtype=np.float32))
    L = 0

    def fm(v, n):
        return f(v).reshape(n, 128).T

    pp = np.concatenate([
        fm(inp["mu_shift"][L], 14), fm(inp["b_merge"][L], 16), fm(inp["b_mlp_up"][L], 32),
        fm(inp["rwkv_w0"][L], 4), fm(inp["rwkv_a0"][L], 4), fm(inp["rwkv_k_k"][L], 4), fm(inp["rwkv_k_a"][L], 4),
        fm(f(inp["rwkv_r_k"][L]).reshape(512), 4), fm(inp["rwkv_gn_w"][L], 4), fm(inp["rwkv_gn_b"][L], 4),
        fm(inp["b_gk"][L], 2)], axis=1)
    rows = np.concatenate([f(inp["ln1_g"][L]), f(inp["ln1_b"][L]), f(inp["ln2_g"][L]), f(inp["ln2_b"][L]),
                           f(inp["b_mlp_down"][L]), f(inp["gla_norm_w"][L])])[None, :]
    return {
        "w_in": f(inp["w_in"][L]), "w_merge": f(inp["w_merge"][L]),
        "w_branch": f(inp["w_branch"][L]).reshape(1024, 1024), "w_out": f(inp["w_out"][L]),
        "w1": f(inp["w_mlp_up"][L]), "w2": f(inp["w_mlp_down"][L]),
        "pp": f(pp), "rows": f(rows), "w_gk_up": f(inp["w_gk_up"][L]),
        "wa_up": f(np.concatenate([f(inp["rwkv_w_up"][L]), f(inp["rwkv_a_up"][L])], axis=0)),
        "g_up": f(inp["rwkv_g_up"][L]),
    }


def prep_core(shared, xb):
    m = dict(shared)
    m["x"] = np.ascontiguousarray(xb, dtype=np.float32)
    m["xT"] = np.ascontiguousarray(np.asarray(xb, dtype=np.float32).T)
    return m


_NC_CACHE = {}


def kernel(**inputs):
    x = np.asarray(inputs["x"], dtype=np.float32)
    B, T, _ = x.shape
    shared = prep_shared(inputs)
    if T not in _NC_CACHE:
        _NC_CACHE[T] = build(T)
    nc = _NC_CACHE[T]
    in_maps = [prep_core(shared, x[b]) for b in range(B)]
    res = run_bass_kernel_spmd(nc, in_maps, core_ids=list(range(B)))
    return np.stack([np.asarray(r["out"], dtype=np.float32) for r in res.results], axis=0)
```

```python
import contextlib
import numpy as np
import concourse.bass as bass
import concourse.mybir as mybir
from concourse.bass_utils import run_bass_kernel_spmd

F32 = mybir.dt.float32
BF16 = mybir.dt.bfloat16
AF = mybir.ActivationFunctionType
ALU = mybir.AluOpType
AX = mybir.AxisListType

D = 1024
KC = 8
DFF = 4096
INW = 3344
C0 = float(np.exp(-0.5))
ALPHA = float(2.0 ** 0.25)
PP_MU, PP_BM, PP_B1, PP_W0, PP_A0, PP_KK, PP_KA, PP_RK, PP_GNW, PP_GNB, PP_BGK, NPP = 0, 14, 30, 62, 66, 70, 74, 78, 82, 86, 90, 92
R_LN1G, R_LN1B, R_LN2G, R_LN2B, R_B2, R_GNW, NR = 0, 1024, 2048, 3072, 4096, 5120, 5248

ENGS = ("pe", "act", "dve", "pool", "sp")


class Prog:
    def __init__(self, nc, stack, same_engine_sync=True):
        self.nc = nc
        self.stack = stack
        self.q = {e: [] for e in ENGS}
        self.cnt = {e: 0 for e in ENGS}
        self.dcnt = {}
        self.sems = {}
        self.waited = {e: {} for e in ENGS}
        self.lastw = {}
        self.readers = {}
        self.same = same_engine_sync
        self.n_ps = 0
        self.nops = 0
        self.max_ops = None
        self.marks = []
        self.sb_bytes = {}

    def sem(self, name):
        if name not in self.sems:
            self.sems[name] = self.stack.enter_context(self.nc.semaphore("s_" + name))
        return self.sems[name]

    def _deps(self, eng, reads, writes):
        need = {}

        def add(ev):
            if ev is None:
                return
            k, v = ev
            if k == eng and (eng == "pe" or (not self.same and eng in ("act", "dve"))):
                return
            if self.waited[eng].get(k, 0) >= v:
                return
            if need.get(k, 0) < v:
                need[k] = v

        for r in reads:
            add(self.lastw.get(r))
        for w in writes:
            add(self.lastw.get(w))
            for k, v in self.readers.get(w, {}).items():
                add((k, v))
        for k, v in need.items():
            self.waited[eng][k] = v
        return list(need.items())

    def _record(self, ev, reads, writes):
        for w in writes:
            self.lastw[w] = ev
            self.readers[w] = {}
        for r in reads:
            d = self.readers.setdefault(r, {})
            if d.get(ev[0], 0) < ev[1]:
                d[ev[0]] = ev[1]

    def op(self, eng, fn, reads=(), writes=(), inc=True):
        self.nops += 1
        if self.max_ops is not None and self.nops > self.max_ops:
            return
        waits = self._deps(eng, reads, writes)
        if inc:
            self.cnt[eng] += 1
            ev = (eng, self.cnt[eng])
        else:
            ev = (eng, self.cnt[eng] + 1)
        self.q[eng].append((waits, fn, eng if inc else None, 1))
        self._record(ev, reads, writes)

    def dma(self, eng, dsem, out, in_, reads=(), writes=(), **kw):
        self.nops += 1
        if self.max_ops is not None and self.nops > self.max_ops:
            return
        waits = self._deps(eng, reads, writes)
        self.dcnt[dsem] = self.dcnt.get(dsem, 0) + 16
        ev = (dsem, self.dcnt[dsem])
        self.q[eng].append((waits, lambda e: e.dma_start(out=out, in_=in_, **kw), dsem, 16))
        self._record(ev, reads, writes)

    def mark(self, name):
        self.marks.append((name, self.nops))

    def barrier(self):
        allv = [(k, v) for k, v in list(self.cnt.items()) + list(self.dcnt.items()) if v > 0]
        for eng in ENGS:
            waits = []
            for k, v in allv:
                if self.waited[eng].get(k, 0) < v:
                    self.waited[eng][k] = v
                    waits.append((k, v))
            self.q[eng].append((waits, None, None, 0))

    def final_wait(self, eng, dsems):
        waits = [(d, self.dcnt[d]) for d in dsems if d in self.dcnt]
        self.q[eng].append((waits, None, None, 0))

    def emit(self):
        nc = self.nc
        for k in list(self.cnt) + list(self.dcnt):
            self.sem(k)
        with nc.Block() as block:
            def mk(eng):
                def body(e):
                    for waits, fn, isem, inc in self.q[eng]:
                        for k, v in waits:
                            e.wait_ge(self.sems[k], v)
                        if fn is None:
                            continue
                        ins = fn(e)
                        if isem is not None:
                            ins.then_inc(self.sems[isem], inc)
                return body
            block.tensor(mk("pe"))
            block.scalar(mk("act"))
            block.vector(mk("dve"))
            block.gpsimd(mk("pool"))
            block.sync(mk("sp"))


def C(name, *args, **kwargs):
    return lambda e: getattr(e, name)(*args, **kwargs)


def bcast(ap, pos, n):
    pat = [list(x) for x in ap.ap]
    pat.insert(pos, [0, n])
    return bass.AP(ap.tensor, ap.offset, pat)


def build(T, same_engine_sync=True, phases=3, max_ops=None, SW=(2, 1)):
    TB = 256
    NBLK = T // TB
    nc = bass.Bass("TRN2", target_bir_lowering=False)

    def dram(name, shape, dt, kind):
        return nc.dram_tensor(name, shape, dt, kind=kind).ap()

    xT_d = dram("xT", [D, T], F32, "ExternalInput")
    x_d = dram("x", [T, D], F32, "ExternalInput")
    w_in_d = dram("w_in", [D, INW], F32, "ExternalInput")
    w_mg_d = dram("w_merge", [D, 2 * D], F32, "ExternalInput")
    w_br_d = dram("w_branch", [D, D], F32, "ExternalInput")
    w_out_d = dram("w_out", [D, D], F32, "ExternalInput")
    w1_d = dram("w1", [D, DFF], F32, "ExternalInput")
    w2_d = dram("w2", [DFF, D], F32, "ExternalInput")
    pp_d = dram("pp", [128, NPP], F32, "ExternalInput")
    rows_d = dram("rows", [1, NR], F32, "ExternalInput")
    wgk_d = dram("w_gk_up", [16, 256], F32, "ExternalInput")
    waup_d = dram("wa_up", [128, 512], F32, "ExternalInput")
    gup_d = dram("g_up", [128, 512], F32, "ExternalInput")
    out_d = dram("out", [T, D], F32, "ExternalOutput")
    oT_d = dram("oT_scr", [D, T], BF16, "Internal")
    x1_d = dram("x1_scr", [T, D], F32, "Internal")

    xT_v = xT_d.rearrange("(kc p) t -> p kc t", p=128)
    oT_v = oT_d.rearrange("(kc p) t -> p kc t", p=128)
    w_in_v = w_in_d.rearrange("(kc p) n -> p kc n", p=128)

    with contextlib.ExitStack() as st:
        P = Prog(nc, st, same_engine_sync)
        P.max_ops = max_ops
        nc._prog = P

        def sbt(stack, name, shape, dt):
            nb = int(np.prod(shape[1:])) * (2 if dt == BF16 else 4)
            P.sb_bytes[id(stack)] = P.sb_bytes.get(id(stack), 0) + ((nb + 31) // 32) * 32
            return stack.enter_context(nc.sbuf_tensor("sb_" + name, shape, dt))

        banks = [st.enter_context(nc.psum_tensor("psb%d" % i, [128, 512], F32)) for i in range(8)]

        def psum():
            i = P.n_ps % 8
            P.n_ps += 1
            return banks[i], ("ps", i)

        def bfv(bank):
            return bank[:, :].bitcast(BF16)

        pp = sbt(st, "pp", [128, NPP], F32)
        dpar = sbt(st, "dpar", [128, 8], F32)
        ident = sbt(st, "ident", [128, 128], BF16)
        ones_bd = sbt(st, "ones_bd", [128, 128], BF16)
        m_su = sbt(st, "m_su", [128, 128], F32)
        m_iu = sbt(st, "m_iu", [128, 128], F32)
        m_sl = sbt(st, "m_sl", [128, 128], F32)
        cmask = sbt(st, "cmask", [128, 128], F32)
        identf = sbt(st, "identf", [128, 128], F32)
        rm128 = sbt(st, "rm128", [128, TB], F32)
        rm64 = sbt(st, "rm64", [128, TB], F32)
        onesf = sbt(st, "onesf", [128, 128], F32)

        P.dma("sp", "d_pp", pp[:, :], pp_d[:, :], writes=["pp"])
        P.op("pool", C("memset", onesf[:, :], 1.0), writes=["onesf"])
        P.op("pool", C("memset", rm128[:, :], 1.0), writes=["rm128"])
        P.op("pool", C("memset", rm64[:, :], 1.0), writes=["rm64"])
        for t0 in range(0, TB, 128):
            P.op("pool", C("memset", rm128[:, t0:t0 + 1], 0.0), writes=["rm128"])
        for t0 in range(0, TB, 64):
            P.op("pool", C("memset", rm64[:, t0:t0 + 1], 0.0), writes=["rm64"])
        P.op("pool", C("affine_select", out=cmask[:, :], in_=onesf[:, :], pattern=[[1, 128]],
                                               compare_op=ALU.is_ge, fill=0.0, base=0, channel_multiplier=-1),
             reads=["onesf"], writes=["cmask"])
        P.op("pool", C("affine_select", out=identf[:, :], in_=onesf[:, :], pattern=[[1, 128]],
                                               compare_op=ALU.is_equal, fill=0.0, base=0, channel_multiplier=-1),
             reads=["onesf"], writes=["identf"])
        P.op("pool", C("tensor_copy", ident[:, :], identf[:, :]), reads=["identf"], writes=["ident"])
        for (mt, cop, name) in ((m_su, ALU.is_gt, "m_su"), (m_iu, ALU.is_ge, "m_iu")):
            P.op("pool", C("memset", mt[:, :], 0.0), writes=[name])
            for hb in (0, 64):
                P.op("pool", C("affine_select",
                    out=mt[hb:hb + 64, hb:hb + 64], in_=onesf[hb:hb + 64, hb:hb + 64], pattern=[[1, 64]],
                    compare_op=cop, fill=0.0, base=0, channel_multiplier=-1), reads=["onesf"], writes=[name])
        P.op("pool", C("memset", m_sl[:, :], 0.0), writes=["m_sl"])
        for hb in (0, 64):
            P.op("pool", C("affine_select",
                out=m_sl[hb:hb + 64, hb:hb + 64], in_=onesf[hb:hb + 64, hb:hb + 64], pattern=[[-1, 64]],
                compare_op=ALU.is_gt, fill=0.0, base=0, channel_multiplier=1), reads=["onesf"], writes=["m_sl"])
        P.op("pool", C("memset", ones_bd[:, :], 0.0), writes=["ones_bd"])
        for hb in (0, 64):
            P.op("pool", C("memset", ones_bd[hb:hb + 64, hb:hb + 64], 1.0), writes=["ones_bd"])
        P.op("dve", C("tensor_scalar", dpar[:, 0:2], pp[:, PP_BGK:PP_BGK + 2], -1.0, None, ALU.mult),
             reads=["pp"], writes=["dpar"])
        P.op("dve", C("tensor_scalar", dpar[:, 2:6], pp[:, PP_KA:PP_KA + 4], -1.0, 1.0, ALU.mult, ALU.add),
             reads=["pp"], writes=["dpar"])

        def ppc(off, i):
            return pp[:, off + i:off + i + 1]

        def mm_group(out_ap, pairs, pskey, reads, inc_last=True):
            n = len(pairs)
            for i, (l, r) in enumerate(pairs):
                P.op("pe", C("matmul", out_ap, l, r, start=(i == 0), stop=(i == n - 1)),
                     reads=reads, writes=[pskey], inc=(i == n - 1) and inc_last)

        if phases >= 1:
          with contextlib.ExitStack() as sa:
            win = sbt(sa, "win", [128, KC, INW], BF16)
            wgk = sbt(sa, "wgk", [16, 256], F32)
            wupP = sbt(sa, "wupP", [128, 512], BF16)
            aupP = sbt(sa, "aupP", [128, 512], BF16)
            gup = sbt(sa, "gup", [128, 512], BF16)
            gnw_bc = sbt(sa, "gnw_bc", [128, 128], F32)
            xTb = [sbt(sa, "xTb%d" % i, [128, KC, TB], BF16) for i in range(2)]
            gklT = sbt(sa, "gklT", [16, TB], F32)
            lsp = sbt(sa, "lsp", [128, 2, TB], F32)
            cum = sbt(sa, "cum", [128, 2, TB], F32)
            eb = lsp
            enb = sbt(sa, "enb", [128, 2, TB], F32)
            qbd = sbt(sa, "qbd", [128, 2, 2, TB], BF16)
            ktT = sbt(sa, "ktT", [128, 2, TB], BF16)
            siluT = sbt(sa, "siluT", [128, 4, TB], BF16)
            vtok = sbt(sa, "vtok", [128, TB // 128, 512], BF16)
            ktok = sbt(sa, "ktok", [128, 256], BF16)
            sT = sbt(sa, "sT", [128, 4, 128], BF16)
            gsq = sbt(sa, "gsq", [128, 4, 128], F32)
            gss = sbt(sa, "gss", [128, 4], F32)
            grs = sbt(sa, "grs", [128, 4], F32)
            gon = sbt(sa, "gon", [128, 4, 128], BF16)
            oaT = [sbt(sa, "oaT", [128, 4, TB], BF16)] * 2
            Sg32 = sbt(sa, "Sg32", [128, 2, 128], F32)
            Sgbf = sbt(sa, "Sgbf", [128, 2, 128], BF16)
            Sgt = sbt(sa, "Sgt", [128, 2, 128], F32)
            hb = [sbt(sa, "hb%d" % i, [128, TB + 1], F32) for i in range(2)]
            dtm = [sbt(sa, "dtm%d" % i, [128, TB], F32) for i in range(2)]
            hprev = sbt(sa, "hprev", [128, 14], F32)
            r32 = sbt(sa, "r32", [128, 4, TB], F32)
            k32 = sbt(sa, "k32", [128, 4, TB], F32)
            vbf = [sbt(sa, "vbf%d" % i, [128, 4, TB], BF16) for i in range(2)]
            lora32 = sbt(sa, "lora32", [128, TB], F32)
            glow32 = sbt(sa, "glow32", [128, TB], F32)
            loraT = sbt(sa, "loraT", [128, TB], BF16)
            sgb = sbt(sa, "sgb", [128, TB], BF16)
            tnames = ["sig", "a32", "cs", "csx", "En", "Ep", "Enx", "kkr", "kk", "kp", "lnss", "inv", "tt"]
            tmp = {n: sbt(sa, "t_" + n, [128, TB], F32) for n in tnames}
            sqb = sbt(sa, "sqb", [128, TB], BF16)
            rkb = sbt(sa, "rkb", [128, TB], BF16)
            ARc = [sbt(sa, "ARc%d" % i, [128, 4, 2, TB], BF16) for i in range(2)]
            BKc = [sbt(sa, "BKc%d" % i, [128, 4, 2, TB], BF16) for i in range(2)]
            pass
            pass
            coef = [sbt(sa, "coef%d" % i, [128, 4, TB], BF16) for i in range(2)]
            gT = [sbt(sa, "gT%d" % i, [128, 4, TB], BF16) for i in range(2)]
            wcl = [sbt(sa, "wcl%d" % i, [128, 4, TB // 64], F32) for i in range(2)]
            ynT = sbt(sa, "ynT", [128, 4, TB], BF16)
            yraw = sbt(sa, "yraw", [128, 4, 128], BF16)
            ysqb = sbt(sa, "ysqb", [128, TB], BF16)
            ym = sbt(sa, "ym", [128, TB], F32)
            yv = sbt(sa, "yv", [128, TB], F32)
            ARbd = [sbt(sa, "ARbd%d" % i, [128, 4, 2, 128], BF16) for i in range(2)]
            BKbd = [sbt(sa, "BKbd%d" % i, [128, 4, 2, 128], BF16) for i in range(2)]
            VTbd = [sbt(sa, "VTbd%d" % i, [128, 4, 128], BF16) for i in range(2)]
            QX = [[sbt(sa, "QX%d_%d" % (b, i), [128, 4, 2, 128], BF16) for i in range(2)] for b in range(2)]
            Pt = [[sbt(sa, "Pt%d_%d" % (b, i), [128, 4, 128], BF16) for i in range(2)] for b in range(2)]
            Aak = [sbt(sa, "Aak%d" % i, [128, 4, 128], BF16) for i in range(2)]
            Arb = [sbt(sa, "Arb%d" % i, [128, 4, 128], BF16) for i in range(2)]
            Ark = [sbt(sa, "Ark%d" % i, [128, 4, 128], BF16) for i in range(2)]
            Btok = [sbt(sa, "Btok%d" % i, [128, 4, 128], BF16) for i in range(2)]
            Ktok = [sbt(sa, "Ktok%d" % i, [128, 4, 128], BF16) for i in range(2)]
            Vtok = [sbt(sa, "Vtok%d" % i, [128, 4, 128], BF16) for i in range(2)]
            RH0 = sbt(sa, "RH0", [128, 4, 128], BF16)
            Ubf = sbt(sa, "Ubf", [128, 4, 128], BF16)
            S32 = sbt(sa, "S32", [128, 4, 128], F32)
            Sbf = sbt(sa, "Sbf", [128, 4, 128], BF16)
            St = sbt(sa, "St", [128, 4, 128], F32)
            obT = [sbt(sa, "obT", [128, 4, TB], BF16)] * 2
            ot1 = sbt(sa, "ot1", [128, TB], F32)
            ot2 = sbt(sa, "ot2", [128, TB], F32)

            P.mark("wloads")
            groups = [(1536, 1552), (0, 512), (512, 1024), (1024, 1536), (1552, 2064), (2064, 2576),
                      (2576, 3088), (3088, 3344)]

            def wkeys(c0, c1):
                return [("win", g) for g, (a, b) in enumerate(groups) if a < c1 and c0 < b]

            P.dma("pool", "d_wgk", wgk[:, :], wgk_d[:, :], writes=["wgk"])
            for g, (c0, c1) in enumerate(groups):
                P.dma("pool", "d_win%d" % g, win[:, :, c0:c1], w_in_v[:, :, c0:c1], writes=[("win", g)])
                if g == 0:
                    P.dma("pool", "d_xT0", xTb[0][:, :, :], xT_v[:, :, 0:TB], writes=[("xTb", 0)])
            P.op("dve", C("memset", wupP[64:128, :], 0.0), writes=["wupP"])
            P.op("dve", C("memset", aupP[0:64, :], 0.0), writes=["aupP"])
            P.op("dve", C("memset", qbd[:, :, :, :], 0.0), writes=["qbd"])
            P.dma("pool", "d_wup", wupP[0:64, :], waup_d[0:64, :], writes=["wupP"])
            P.dma("pool", "d_aup", aupP[64:128, :], waup_d[64:128, :], writes=["aupP"])
            P.dma("pool", "d_gup", gup[:, :], gup_d[:, :], writes=["gup"])
            P.dma("sp", "d_gnw", gnw_bc[:, :], rows_d[0:1, R_GNW:R_GNW + 128].to_broadcast([128, 128]), writes=["gnw_bc"])

            P.mark("stateinit")
            P.op("dve", C("memset", Sg32[:, :, :], 0.0), writes=["Sg32"])
            P.op("dve", C("memset", Sgbf[:, :, :], 0.0), writes=["Sgbf"])
            P.op("dve", C("memset", S32[:, :, :], 0.0), writes=["S32"])
            P.op("dve", C("memset", Sbf[:, :, :], 0.0), writes=["Sbf"])
            P.op("dve", C("memset", hprev[:, :], 0.0), writes=[("hprev", i) for i in range(14)])
            for i in range(2):
                P.op("dve", C("memset", ARbd[i][:, :, :, :], 0.0), writes=[("ARbd", i)])
                P.op("dve", C("memset", BKbd[i][:, :, :, :], 0.0), writes=[("BKbd", i)])
                P.op("dve", C("memset", VTbd[i][:, :, :], 0.0), writes=[("VTbd", i)])

            def proj_fm(blk, c0, ncols, ):
                xs = blk % 2
                bank, pk = psum()
                outp = bank[0:ncols, 0:TB]
                mm_group(outp, [(win[:, kc, c0:c0 + ncols], xTb[xs][:, kc, :]) for kc in range(KC)], pk,
                         reads=wkeys(c0, c0 + ncols) + [("xTb", xs)])
                return outp, pk

            chunk_ctr = [0]

            def emit_block(blk, part):
                xs = blk % 2
                t0 = blk * TB
                if part == "front_gla":
                    P.mark("gla_gate")
                    o, pk = proj_fm(blk, 1536, 16)
                    P.op("act", C("activation", out=gklT[:, :], in_=o, func=AF.Copy), reads=[pk], writes=["gklT"])
                    bank, pk = psum()
                    for c in range(2):
                        mm_group(bank[:, c * TB:(c + 1) * TB], [(wgk[0:16, c * 128:(c + 1) * 128], gklT[0:16, :])], pk,
                                 reads=["wgk", "gklT"])
                    for c in range(2):
                        P.op("act", C("activation", out=lsp[:, c, :], in_=bank[:, c * TB:(c + 1) * TB],
                                                                             func=AF.Exp, bias=dpar[:, c:c + 1], scale=-1.0),
                             reads=[pk, "dpar"], writes=["lsp"])
                    P.op("act", C("activation", out=lsp[:, :, :], in_=lsp[:, :, :], func=AF.Ln, bias=1.0, scale=1.0),
                         reads=["lsp"], writes=["lsp"])
                    for c in range(2):
                        P.op("dve", C("tensor_tensor_scan", cum[:, c, :], rm128[:, :], lsp[:, c, :], 0.0, ALU.mult, ALU.add),
                             reads=["lsp", "rm128"], writes=["cum"])
                    P.op("act", C("activation", out=eb[:, :, :], in_=cum[:, :, :], func=AF.Exp, scale=-1.0 / 16.0),
                         reads=["cum"], writes=["lsp"])
                    P.op("act", C("activation", out=enb[:, :, :], in_=cum[:, :, :], func=AF.Exp, scale=1.0 / 16.0),
                         reads=["cum"], writes=["enb"])
                    yield
                    P.mark("gla_qk")
                    for c in range(2):
                        o, pk = proj_fm(blk, c * 128, 128)
                        for hh in range(2):
                            hr = slice(hh * 64, hh * 64 + 64)
                            P.op("dve", C("scalar_tensor_tensor", out=qbd[hr, c, hh, :], in0=o[hr, :], scalar=0.125, in1=eb[hr, c, :],
                                          op0=ALU.mult, op1=ALU.mult), reads=[pk, "lsp"], writes=["qbd"])
                    for c in range(2):
                        o, pk = proj_fm(blk, 256 + c * 128, 128)
                        P.op("dve", C("tensor_tensor", out=ktT[:, c, :], in0=o, in1=enb[:, c, :], op=ALU.mult),
                             reads=[pk, "enb"], writes=["ktT"])
                        yield
                    P.mark("gla_silu")
                    for c in range(4):
                        o, pk = proj_fm(blk, 1024 + c * 128, 128)
                        P.op("act", C("activation", out=siluT[:, c, :], in_=o, func=AF.Silu),
                             reads=[pk], writes=["siluT"])
                        yield
                    P.mark("gla_v")
                    for j in range(TB // 128):
                        bank, pk = psum()
                        mm_group(bank[:, :], [(xTb[xs][:, kc, j * 128:(j + 1) * 128], win[:, kc, 512:1024]) for kc in range(KC)], pk,
                                 reads=wkeys(512, 1024) + [("xTb", xs)])
                        P.op("act", C("activation", out=vtok[:, j, :], in_=bank[:, :], func=AF.Copy),
                             reads=[pk], writes=[("vtok", j)])
                        yield

                if part == "gla_core":
                    P.mark("gla_core")
                    for j in range(TB // 128):
                        ts = slice(j * 128, (j + 1) * 128)
                        bank, pk = psum()
                        bv = bfv(bank)
                        for c in range(2):
                            P.op("pe", C("transpose", bv[:, c * 128:(c + 1) * 128], ktT[:, c, ts], ident[:, :]),
                                 reads=["ktT", "ident"], writes=[pk], inc=(c == 1))
                        P.op("dve", C("tensor_copy", ktok[:, :], bv[:, 0:256]), reads=[pk], writes=["ktok"])
                        yield
                        bank_s, pks = psum()
                        for c in range(2):
                            P.op("pe", C("matmul", bank_s[:, c * 256:(c + 1) * 256], ktT[:, c, ts], qbd[:, c, :, ts], start=True, stop=True),
                                 reads=["ktT", "qbd"], writes=[pks], inc=(c == 1))
                        P.op("dve", C("tensor_tensor",
                            out=sT[:, :, :], in0=bank_s[:, :].rearrange("p (h i) -> p h i", h=4), in1=bcast(cmask[:, :], 1, 4), op=ALU.mult),
                            reads=[pks, "cmask"], writes=["sT"])
                        yield
                        bank_o, pko = psum()
                        for h in range(4):
                            hr = slice((h % 2) * 64, (h % 2) * 64 + 64)
                            c = h // 2
                            P.op("pe", C("matmul",
                                bank_o[:, h * 128:(h + 1) * 128], sT[:, h, :], vtok[:, j, h * 128:(h + 1) * 128], start=True, stop=False),
                                reads=["sT", ("vtok", j)], writes=[pko], inc=False)
                            P.op("pe", C("matmul", bank_o[:, h * 128:(h + 1) * 128], qbd[:, c, h % 2, ts], Sgbf[:, c, :], start=False, stop=True),
                                 reads=["qbd", "Sgbf"], writes=[pko], inc=(h == 3))
                        bank_u, pku = psum()
                        for c in range(2):
                            P.op("pe", C("matmul",
                                bank_u[:, c * 256:(c + 1) * 256], ktok[:, c * 128:(c + 1) * 128], vtok[:, j, c * 256:(c + 1) * 256],
                                start=True, stop=True), reads=["ktok", ("vtok", j)], writes=[pku], inc=(c == 1))
                        lastc = j * 128 + 127
                        for hh in range(2):
                            hr = slice(hh * 64, hh * 64 + 64)
                            buv = bank_u[hr, :].rearrange("q (c h v) -> q c h v", c=2, h=2)[:, :, hh, :]
                            P.op("dve", C("tensor_tensor", out=Sgt[hr, :, :], in0=buv, in1=Sg32[hr, :, :], op=ALU.add),
                                 reads=[pku, "Sg32"], writes=["Sgt"])
                            P.op("dve", C("tensor_tensor", out=Sg32[hr, :, :], in0=Sgt[hr, :, :], in1=bcast(eb[hr, :, lastc], 2, 128), op=ALU.mult),
                                 reads=["Sgt", "lsp"], writes=["Sg32"])
                        P.op("act", C("activation", out=Sgbf[:, :, :], in_=Sg32[:, :, :], func=AF.Copy),
                             reads=["Sg32"], writes=["Sgbf"])
                        P.op("act", C("activation", out=gsq[:, :, :], in_=bank_o[:, :].rearrange("p (h i) -> p h i", h=4),
                                                                          func=AF.Square), reads=[pko], writes=["gsq"])
                        P.op("dve", C("tensor_reduce", out=gss[:, :], in_=gsq[:, :, :], axis=AX.X, op=ALU.add),
                             reads=["gsq"], writes=["gss"])
                        P.op("act", C("activation", out=grs[:, :], in_=gss[:, :], func=AF.Ln, bias=1e-5, scale=1.0 / 128.0),
                             reads=["gss"], writes=["grs"])
                        P.op("act", C("activation", out=grs[:, :], in_=grs[:, :], func=AF.Exp, scale=-0.5),
                             reads=["grs"], writes=["grs"])
                        P.op("dve", C("tensor_tensor", out=gsq[:, :, :], in0=bank_o[:, :].rearrange("p (h i) -> p h i", h=4), in1=bcast(grs[:, :], 2, 128), op=ALU.mult),
                             reads=[pko, "grs", "gsq"], writes=["gsq"])
                        P.op("dve", C("tensor_tensor", out=gon[:, :, :], in0=gsq[:, :, :], in1=bcast(gnw_bc[:, :], 1, 4), op=ALU.mult),
                             reads=["gsq", "gnw_bc"], writes=["gon"])
                        yield
                        bank_t, pkt = psum()
                        bvt = bfv(bank_t)
                        for h in range(4):
                            P.op("pe", C("transpose", bvt[:, h * 128:(h + 1) * 128], gon[:, h, :], ident[:, :]),
                                 reads=["gon", "ident"], writes=[pkt], inc=(h == 3))
                        P.op("dve", C("tensor_tensor",
                            out=oaT[xs][:, :, ts], in0=bvt[:, 0:512].rearrange("p (h i) -> p h i", h=4), in1=siluT[:, :, ts], op=ALU.mult),
                            reads=[pkt, "siluT"], writes=["oaT"])
                    P.dma("sp", "d_scrA", oT_v[:, 0:4, t0:t0 + TB], oaT[xs][:, :, :], reads=["oaT"],
                          writes=[("oTd", blk, 0)])


                    yield
                if part == "rw_front":
                    P.mark("rw_proj")
                    def rw_chunk(cidx, dst_fn, dkeys):
                        hs = cidx % 2
                        o, pk = proj_fm(blk, 1552 + cidx * 128, 128)
                        P.op("act", C("activation", out=hb[hs][:, 1:TB + 1], in_=o, func=AF.Copy),
                             reads=[pk], writes=[("hb", hs)])
                        P.op("pool", C("tensor_copy", hb[hs][:, 0:1], hprev[:, cidx:cidx + 1]),
                             reads=[("hprev", cidx)], writes=[("hb", hs)])
                        P.op("pool", C("tensor_copy", hprev[:, cidx:cidx + 1], hb[hs][:, TB:TB + 1]),
                             reads=[("hb", hs)], writes=[("hprev", cidx)])
                        P.op("dve", C("tensor_tensor", out=dtm[hs][:, :], in0=hb[hs][:, 0:TB], in1=hb[hs][:, 1:TB + 1], op=ALU.subtract),
                             reads=[("hb", hs)], writes=[("dtm", hs)])
                        P.op("dve", C("scalar_tensor_tensor", out=dst_fn(), in0=dtm[hs][:, :], scalar=ppc(PP_MU, cidx),
                                                                     in1=hb[hs][:, 1:TB + 1], op0=ALU.mult, op1=ALU.add),
                             reads=[("dtm", hs), ("hb", hs), "pp"], writes=dkeys)

                    for p in range(4):
                        rw_chunk(p, lambda p=p: r32[:, p, :], [("r32", p)])
                        yield
                    for p in range(4):
                        rw_chunk(4 + p, lambda p=p: k32[:, p, :], [("k32", p)])
                        yield
                    for p in range(4):
                        rw_chunk(8 + p, lambda p=p: vbf[xs][:, p, :], [("vbf", xs, p)])
                        yield
                    rw_chunk(12, lambda: lora32[:, :], ["lora32"])
                    rw_chunk(13, lambda: glow32[:, :], ["glow32"])
                    P.mark("rw_lora")
                    P.op("act", C("activation", out=loraT[0:64, :], in_=lora32[0:64, :], func=AF.Tanh),
                         reads=["lora32"], writes=["loraT"])
                    P.op("act", C("activation", out=loraT[64:128, :], in_=lora32[64:128, :], func=AF.Copy),
                         reads=["lora32"], writes=["loraT"])
                    P.op("act", C("activation", out=sgb[:, :], in_=glow32[:, :], func=AF.Sigmoid),
                         reads=["glow32"], writes=["sgb"])
                    for p in range(4):
                        ps_ = slice(p * 128, (p + 1) * 128)
                        b1, k1 = psum()
                        mm_group(b1[:, 0:TB], [(wupP[:, ps_], loraT[:, :])], k1, reads=["wupP", "loraT"])
                        b2, k2 = psum()
                        mm_group(b2[:, 0:TB], [(aupP[:, ps_], loraT[:, :])], k2, reads=["aupP", "loraT"])
                        b3, k3 = psum()
                        mm_group(b3[:, 0:TB], [(gup[:, ps_], sgb[:, :])], k3, reads=["gup", "sgb"])
                        P.op("dve", C("tensor_scalar", tmp["kkr"][:, :], k32[:, p, :], ppc(PP_KK, p), None, ALU.mult),
                             reads=[("k32", p), "pp"], writes=["kkr"])
                        P.op("act", C("activation", out=tmp["sig"][:, :], in_=b1[:, 0:TB], func=AF.Sigmoid, bias=ppc(PP_W0, p), scale=1.0),
                             reads=[k1, "pp"], writes=["sig"])
                        P.op("act", C("activation", out=tmp["a32"][:, :], in_=b2[:, 0:TB], func=AF.Sigmoid, bias=ppc(PP_A0, p), scale=1.0),
                             reads=[k2, "pp"], writes=["a32"])
                        P.op("act", C("activation", out=sqb[:, :], in_=tmp["kkr"][:, :], func=AF.Square), reads=["kkr"], writes=["sqb"])
                        P.op("act", C("activation", out=gT[xs][:, p, :], in_=b3[:, 0:TB], func=AF.Copy), reads=[k3], writes=[("gT", xs, p)])
                        yield
                        b4, k4 = psum()
                        mm_group(b4[:, 0:TB], [(ones_bd[:, :], sqb[:, :])], k4, reads=["ones_bd", "sqb"])
                        P.op("dve", C("tensor_tensor_scan", tmp["cs"][:, :], rm64[:, :], tmp["sig"][:, :], 0.0, ALU.mult, ALU.add),
                             reads=["sig", "rm64"], writes=["cs"])
                        P.op("dve", C("tensor_scalar", tmp["tt"][:, :], tmp["a32"][:, :], ppc(PP_KA, p), dpar[:, 2 + p:3 + p], ALU.mult, ALU.add),
                             reads=["a32", "pp", "dpar"], writes=["tt"])
                        P.op("dve", C("tensor_tensor", out=tmp["csx"][:, :], in0=tmp["cs"][:, :], in1=tmp["sig"][:, :], op=ALU.subtract),
                             reads=["cs", "sig"], writes=["csx"])
                        P.op("dve", C("tensor_tensor", out=tmp["kp"][:, :], in0=k32[:, p, :], in1=tmp["tt"][:, :], op=ALU.mult),
                             reads=[("k32", p), "tt"], writes=["kp"])
                        P.op("act", C("activation", out=tmp["lnss"][:, :], in_=b4[:, 0:TB], func=AF.Ln, bias=1e-24, scale=1.0),
                             reads=[k4], writes=["lnss"])
                        P.op("act", C("activation", out=tmp["En"][:, :], in_=tmp["cs"][:, :], func=AF.Exp, scale=-C0), reads=["cs"], writes=["En"])
                        P.op("act", C("activation", out=tmp["inv"][:, :], in_=tmp["lnss"][:, :], func=AF.Exp, scale=-0.5), reads=["lnss"], writes=["inv"])
                        P.op("act", C("activation", out=tmp["Enx"][:, :], in_=tmp["csx"][:, :], func=AF.Exp, scale=-C0), reads=["csx"], writes=["Enx"])
                        P.op("act", C("activation", out=tmp["Ep"][:, :], in_=tmp["cs"][:, :], func=AF.Exp, scale=C0), reads=["cs"], writes=["Ep"])
                        P.op("dve", C("scalar_tensor_tensor", out=rkb[:, :], in0=r32[:, p, :], scalar=ppc(PP_RK, p), in1=tmp["kp"][:, :],
                                      op0=ALU.mult, op1=ALU.mult), reads=[("r32", p), "pp", "kp"], writes=["rkb"])
                        yield
                        b5, k5 = psum()
                        mm_group(b5[:, 0:TB], [(ones_bd[:, :], rkb[:, :])], k5, reads=["ones_bd", "rkb"])
                        P.op("dve", C("tensor_tensor", out=tmp["kk"][:, :], in0=tmp["kkr"][:, :], in1=tmp["inv"][:, :], op=ALU.mult),
                             reads=["kkr", "inv"], writes=["kk"])
                        P.op("dve", C("tensor_copy", wcl[xs][:, p, :], tmp["En"][:, 63:TB:64]), reads=["En"], writes=[("wcl", xs, p)])
                        P.op("dve", C("scalar_tensor_tensor", out=ARc[xs][:, p, 0, :], in0=tmp["kk"][:, :], scalar=-1.0, in1=tmp["Enx"][:, :],
                                      op0=ALU.mult, op1=ALU.mult), reads=["kk", "Enx"], writes=[("ARc", xs, p)])
                        P.op("act", C("activation", out=coef[xs][:, p, :], in_=b5[:, 0:TB], func=AF.Copy), reads=[k5], writes=[("coef", xs, p)])
                        P.op("dve", C("tensor_tensor", out=tmp["csx"][:, :], in0=tmp["kk"][:, :], in1=tmp["a32"][:, :], op=ALU.mult),
                             reads=["kk", "a32"], writes=["csx"])
                        P.op("dve", C("tensor_tensor", out=BKc[xs][:, p, 1, :], in0=tmp["kp"][:, :], in1=tmp["Ep"][:, :], op=ALU.mult),
                             reads=["kp", "Ep"], writes=[("BKc", xs, p)])
                        P.op("dve", C("tensor_tensor", out=BKc[xs][:, p, 0, :], in0=tmp["csx"][:, :], in1=tmp["Ep"][:, :], op=ALU.mult),
                             reads=["csx", "Ep"], writes=[("BKc", xs, p)])
                        P.op("dve", C("tensor_tensor", out=ARc[xs][:, p, 1, :], in0=r32[:, p, :], in1=tmp["En"][:, :], op=ALU.mult),
                             reads=[("r32", p), "En"], writes=[("ARc", xs, p)])
                        yield

                    yield
                if part == "chunks":
                    allp = lambda n: [(n, xs, p) for p in range(4)] if n in ("ARc", "BKc", "vbf", "wcl", "coef", "gT") else [(n, p) for p in range(4)]
                    P.mark("rw_chunks")

                    def v4(bank_, half):
                        return bank_[:, :].rearrange("q (p a t) -> q p a t", p=2, a=2)[:, :, half, :]

                    def v3(bank_):
                        return bank_[:, :].rearrange("q (p t) -> q p t", p=4)

                    def prep(c):
                        cs_ = slice(c * 64, (c + 1) * 64)
                        bs = c % 2
                        AR, BK, VT = ARbd[bs], BKbd[bs], VTbd[bs]
                        kAR, kBK, kVT = ("ARbd", bs), ("BKbd", bs), ("VTbd", bs)
                        for hh in range(2):
                            hr = slice(hh * 64, hh * 64 + 64)
                            if hh == 0:
                                P.op("act", C("activation", out=AR[hr, :, :, hr], in_=ARc[xs][hr, :, :, cs_], func=AF.Copy), reads=allp("ARc"), writes=[kAR])
                            else:
                                P.op("dve", C("tensor_copy", AR[hr, :, :, hr], ARc[xs][hr, :, :, cs_]), reads=allp("ARc"), writes=[kAR])
                            pass
                            if hh == 1:
                                P.op("act", C("activation", out=BK[hr, :, :, hr], in_=BKc[xs][hr, :, :, cs_], func=AF.Copy), reads=allp("BKc"), writes=[kBK])
                            else:
                                P.op("dve", C("tensor_copy", BK[hr, :, :, hr], BKc[xs][hr, :, :, cs_]), reads=allp("BKc"), writes=[kBK])
                            pass
                            P.op("pool", C("tensor_copy", VT[hr, :, hr], vbf[xs][hr, :, cs_]), reads=allp("vbf"), writes=[kVT])
                        yield
                        bQ0, kQ0 = psum()
                        bRb, kRb = psum()
                        b3, k3 = psum()
                        for p in range(4):
                            P.op("pe", C("matmul", bQ0[:, p * 128:(p + 1) * 128], BK[:, p, 0, :], AR[:, p, 0, :], start=True, stop=True),
                                 reads=[kBK, kAR], writes=[kQ0], inc=(p == 3))
                        for p in range(4):
                            P.op("pe", C("matmul", b3[:, p * 128:(p + 1) * 128], AR[:, p, 0, :], BK[:, p, 0, :], start=True, stop=True),
                                 reads=[kBK, kAR], writes=[k3], inc=(p == 3))
                        for p in range(4):
                            P.op("pe", C("matmul", bRb[:, p * 128:(p + 1) * 128], BK[:, p, 0, :], AR[:, p, 1, :], start=True, stop=True),
                                 reads=[kBK, kAR], writes=[kRb], inc=(p == 3))
                        qs = 0
                        ps0 = 0
                        QXs, Pts = QX[bs], Pt[bs]
                        kQX = lambda i: ("QX", bs, i)
                        kPt = lambda i: ("Pt", bs, i)
                        P.op("dve", C("tensor_tensor", out=QXs[qs][:, :, 0, :], in0=v3(bQ0), in1=bcast(m_su[:, :], 1, 4), op=ALU.mult),
                             reads=[kQ0, "m_su"], writes=[kQX(qs)])
                        P.op("dve", C("tensor_tensor", out=Pts[ps0][:, :, :], in0=v3(b3), in1=bcast(m_sl[:, :], 1, 4), op=ALU.mult),
                             reads=[k3, "m_sl"], writes=[kPt(ps0)])
                        P.op("dve", C("tensor_tensor", out=QXs[1 - qs][:, :, 1, :], in0=QXs[qs][:, :, 0, :], in1=bcast(identf[:, :], 1, 4), op=ALU.add),
                             reads=[kQX(qs), "identf"], writes=[kQX(1 - qs)])
                        P.op("dve", C("tensor_tensor", out=Arb[bs][:, :, :], in0=v3(bRb), in1=bcast(m_iu[:, :], 1, 4), op=ALU.mult),
                             reads=[kRb, "m_iu"], writes=[("Arb", bs)])
                        yield
                        bAk, kAk = psum()
                        bRk, kRk = psum()
                        for p in range(4):
                            P.op("pe", C("matmul", bAk[:, p * 128:(p + 1) * 128], BK[:, p, 1, :], AR[:, p, 0, :], start=True, stop=True),
                                 reads=[kBK, kAR], writes=[kAk], inc=(p == 3))
                        for p in range(4):
                            P.op("pe", C("matmul", bRk[:, p * 128:(p + 1) * 128], BK[:, p, 1, :], AR[:, p, 1, :], start=True, stop=True),
                                 reads=[kBK, kAR], writes=[kRk], inc=(p == 3))
                        P.op("dve", C("tensor_tensor", out=Aak[bs][:, :, :], in0=v3(bAk), in1=bcast(m_su[:, :], 1, 4), op=ALU.mult),
                             reads=[kAk, "m_su"], writes=[("Aak", bs)])
                        P.op("dve", C("tensor_tensor", out=Ark[bs][:, :, :], in0=v3(bRk), in1=bcast(m_iu[:, :], 1, 4), op=ALU.mult),
                             reads=[kRk, "m_iu"], writes=[("Ark", bs)])
                        yield
                        for lvl in range(6):
                            Pn, Qn = Pts[ps0], QXs[qs]
                            kP, kQ = kPt(ps0), kQX(qs)
                            Pnew, Qnew = Pts[1 - ps0], QXs[1 - qs]
                            kPn, kQn = kPt(1 - ps0), kQX(1 - qs)
                            if lvl <= 4:
                                bp, kp_ = psum()
                                for p in range(4):
                                    P.op("pe", C("matmul", bp[:, p * 128:(p + 1) * 128], Qn[:, p, 0, :], Pn[:, p, :], start=True, stop=True),
                                         reads=[kP, kQ], writes=[kp_], inc=(p == 3))
                            if lvl == 0:
                                bq, kq_ = psum()
                                for p in range(4):
                                    P.op("pe", C("matmul", bq[:, p * 128:(p + 1) * 128], Pn[:, p, :], Qn[:, p, 0, :], start=True, stop=True),
                                         reads=[kP, kQ], writes=[kq_], inc=(p == 3))
                                P.op("dve", C("tensor_copy", Qnew[:, :, 0, :], v3(bq)), reads=[kq_], writes=[kQn])
                                pass
                            elif lvl <= 3:
                                bq, kq_ = psum()
                                bx, kx_ = psum()
                                for p in range(4):
                                    P.op("pe", C("matmul", bq[:, p * 128:(p + 1) * 128], Pn[:, p, :], Qn[:, p, 0, :], start=True, stop=True),
                                         reads=[kP, kQ], writes=[kq_], inc=(p == 3))
                                for p in range(4):
                                    P.op("pe", C("matmul", bx[:, p * 128:(p + 1) * 128], Pn[:, p, :], Qn[:, p, 1, :], start=True, stop=True),
                                         reads=[kP, kQ], writes=[kx_], inc=(p == 3))
                                P.op("act", C("activation", out=Qnew[:, :, 0, :], in_=v3(bq), func=AF.Copy), reads=[kq_], writes=[kQn])
                                P.op("dve", C("tensor_tensor", out=Qnew[:, :, 1, :], in0=v3(bx), in1=Qn[:, :, 1, :], op=ALU.add),
                                     reads=[kx_, kQ], writes=[kQn])
                            else:
                                bq, kq_ = psum()
                                for p in range(4):
                                    P.op("pe", C("matmul", bq[:, p * 128:(p + 1) * 128], Pn[:, p, :], Qn[:, p, 1, :], start=True, stop=True),
                                         reads=[kP, kQ], writes=[kq_], inc=(p == 3))
                                P.op("dve", C("tensor_tensor", out=Qnew[:, :, 1, :], in0=v3(bq), in1=Qn[:, :, 1, :], op=ALU.add),
                                     reads=[kq_, kQ], writes=[kQn])
                            if lvl <= 4:
                                P.op("act", C("activation", out=Pnew[:, :, :], in_=v3(bp), func=AF.Copy), reads=[kp_], writes=[kPn])
                            ps0, qs = 1 - ps0, 1 - qs
                            yield
                        assert qs == 0
                        for (src_fn, dst, dkey, skey, eng_) in ((lambda p: BK[:, p, 0, :], Btok[bs], ("Btok", bs), kBK, "act"),
                                                                (lambda p: BK[:, p, 1, :], Ktok[bs], ("Ktok", bs), kBK, "dve"),
                                                                (lambda p: VT[:, p, :], Vtok[bs], ("Vtok", bs), kVT, "act")):
                            bt, kt = psum()
                            btv = bfv(bt)
                            for p in range(4):
                                P.op("pe", C("transpose", btv[:, p * 128:(p + 1) * 128], src_fn(p), ident[:, :]),
                                     reads=[skey, "ident"], writes=[kt], inc=(p == 3))
                            srcv = btv[:, 0:512].rearrange("q (p t) -> q p t", p=4)
                            if eng_ == "act":
                                P.op("act", C("activation", out=dst[:, :, :], in_=srcv, func=AF.Copy), reads=[kt], writes=[dkey])
                            else:
                                P.op("dve", C("tensor_copy", dst[:, :, :], srcv), reads=[kt], writes=[dkey])
                            yield

                    def post(c):
                        cs_ = slice(c * 64, (c + 1) * 64)
                        bs = c % 2
                        AR = ARbd[bs]
                        kAR = ("ARbd", bs)
                        XT, kXT = QX[bs][0], ("QX", bs, 0)
                        br, kr = psum()
                        for p in range(4):
                            P.op("pe", C("matmul", br[:, p * 128:(p + 1) * 128], AR[:, p, 0, :], Sbf[:, p, :], start=True, stop=False),
                                 reads=[kAR, "Sbf"], writes=[kr], inc=False)
                            P.op("pe", C("matmul", br[:, p * 128:(p + 1) * 128], Aak[bs][:, p, :], Vtok[bs][:, p, :], start=False, stop=True),
                                 reads=[("Aak", bs), ("Vtok", bs)], writes=[kr], inc=(p == 3))
                        P.op("act", C("activation", out=RH0[:, :, :], in_=v3(br), func=AF.Copy), reads=[kr], writes=["RH0"])
                        yield
                        bu, ku = psum()
                        for p in range(4):
                            P.op("pe", C("matmul", bu[:, p * 128:(p + 1) * 128], XT[:, p, 1, :], RH0[:, p, :], start=True, stop=True),
                                 reads=[kXT, "RH0"], writes=[ku], inc=(p == 3))
                        P.op("act", C("activation", out=Ubf[:, :, :], in_=v3(bu), func=AF.Copy), reads=[ku], writes=["Ubf"])
                        yield
                        by, ky = psum()
                        for p in range(4):
                            P.op("pe", C("matmul", by[:, p * 128:(p + 1) * 128], AR[:, p, 1, :], Sbf[:, p, :], start=True, stop=False),
                                 reads=[kAR, "Sbf"], writes=[ky], inc=False)
                            P.op("pe", C("matmul", by[:, p * 128:(p + 1) * 128], Arb[bs][:, p, :], Ubf[:, p, :], start=False, stop=False),
                                 reads=[("Arb", bs), "Ubf"], writes=[ky], inc=False)
                            P.op("pe", C("matmul", by[:, p * 128:(p + 1) * 128], Ark[bs][:, p, :], Vtok[bs][:, p, :], start=False, stop=True),
                                 reads=[("Ark", bs), ("Vtok", bs)], writes=[ky], inc=(p == 3))
                        bs_, ks_ = psum()
                        for p in range(4):
                            P.op("pe", C("matmul", bs_[:, p * 128:(p + 1) * 128], Btok[bs][:, p, :], Ubf[:, p, :], start=True, stop=False),
                                 reads=[("Btok", bs), "Ubf"], writes=[ks_], inc=False)
                            P.op("pe", C("matmul", bs_[:, p * 128:(p + 1) * 128], Ktok[bs][:, p, :], Vtok[bs][:, p, :], start=False, stop=True),
                                 reads=[("Ktok", bs), ("Vtok", bs)], writes=[ks_], inc=(p == 3))
                        P.op("dve", C("tensor_tensor", out=St[:, :, :], in0=v3(bs_), in1=S32[:, :, :], op=ALU.add),
                             reads=[ks_, "S32"], writes=["St"])
                        P.op("dve", C("tensor_tensor", out=S32[:, :, :], in0=St[:, :, :], in1=bcast(wcl[xs][:, :, c], 2, 128), op=ALU.mult),
                             reads=["St"] + allp("wcl"), writes=["S32"])
                        P.op("dve", C("tensor_copy", Sbf[:, :, :], S32[:, :, :]), reads=["S32"], writes=["Sbf"])
                        P.op("act", C("activation", out=yraw[:, :, :], in_=v3(by), func=AF.Copy), reads=[ky], writes=["yraw"])
                        yield
                        bt, kt = psum()
                        btv = bfv(bt)
                        for p in range(4):
                            P.op("pe", C("transpose", btv[:, p * 128:(p + 1) * 128], yraw[:, p, :], ident[:, :]),
                                 reads=["yraw", "ident"], writes=[kt], inc=(p == 3))
                        for hh in range(2):
                            hr = slice(hh * 64, hh * 64 + 64)
                            P.op("act", C("activation", out=ynT[hr, :, cs_], in_=btv[hr, 0:512].rearrange("q (p t) -> q p t", p=4)[:, :, hr],
                                          func=AF.Copy), reads=[kt], writes=allp("ynT"))
                        yield

                    NCK = TB // 64
                    for c0 in range(0, NCK, 2):
                        yield from ileave(prep(c0), prep(c0 + 1))
                        yield from post(c0)
                        yield from post(c0 + 1)
                    P.mark("rw_out")
                    for p in range(4):
                        b1_, k1_ = psum()
                        mm_group(b1_[:, 0:TB], [(ones_bd[:, :], ynT[:, p, :])], k1_, reads=["ones_bd", ("ynT", p)])
                        P.op("act", C("activation", out=ysqb[:, :], in_=ynT[:, p, :], func=AF.Square), reads=[("ynT", p)], writes=["ysqb"])
                        b2_, k2_ = psum()
                        mm_group(b2_[:, 0:TB], [(ones_bd[:, :], ysqb[:, :])], k2_, reads=["ones_bd", "ysqb"])
                        P.op("act", C("activation", out=ym[:, :], in_=b1_[:, 0:TB], func=AF.Copy, scale=1.0 / 64.0), reads=[k1_], writes=["ym"])
                        P.op("dve", C("tensor_tensor", out=yv[:, :], in0=ym[:, :], in1=ym[:, :], op=ALU.mult), reads=["ym"], writes=["yv"])
                        P.op("dve", C("scalar_tensor_tensor", out=yv[:, :], in0=b2_[:, 0:TB], scalar=1.0 / 64.0, in1=yv[:, :], op0=ALU.mult, op1=ALU.subtract),
                             reads=[k2_, "yv"], writes=["yv"])
                        P.op("act", C("activation", out=yv[:, :], in_=yv[:, :], func=AF.Ln, bias=64e-5, scale=1.0), reads=["yv"], writes=["yv"])
                        P.op("act", C("activation", out=yv[:, :], in_=yv[:, :], func=AF.Exp, scale=-0.5), reads=["yv"], writes=["yv"])
                        P.op("dve", C("tensor_tensor", out=ot1[:, :], in0=ynT[:, p, :], in1=ym[:, :], op=ALU.subtract), reads=[("ynT", p), "ym"], writes=["ot1"])
                        P.op("dve", C("tensor_tensor", out=ot1[:, :], in0=ot1[:, :], in1=yv[:, :], op=ALU.mult), reads=["ot1", "yv"], writes=["ot1"])
                        P.op("dve", C("tensor_tensor", out=ot2[:, :], in0=coef[xs][:, p, :], in1=vbf[xs][:, p, :], op=ALU.mult), reads=[("coef", xs, p), ("vbf", xs, p)], writes=["ot2"])
                        P.op("dve", C("tensor_scalar", ot1[:, :], ot1[:, :], ppc(PP_GNW, p), ppc(PP_GNB, p), ALU.mult, ALU.add),
                             reads=["ot1", "pp"], writes=["ot1"])
                        P.op("dve", C("tensor_tensor", out=ot1[:, :], in0=ot1[:, :], in1=ot2[:, :], op=ALU.add),
                             reads=["ot1", "ot2"], writes=["ot1"])
                        P.op("dve", C("tensor_tensor", out=obT[xs][:, p, :], in0=ot1[:, :], in1=gT[xs][:, p, :], op=ALU.mult), reads=["ot1", ("gT", xs, p)], writes=["obT"])
                        yield
                    P.dma("sp", "d_scrB", oT_v[:, 4:8, t0:t0 + TB], obT[xs][:, :, :], reads=["obT"],
                          writes=[("oTd", blk, 1)])

            def run(g):
                for _ in g:
                    pass

            def ileave(*gens, weights=None):
                gens = list(gens)
                w = {id(g): (weights[i] if weights else 1) for i, g in enumerate(gens)}
                while gens:
                    for g in list(gens):
                        for _ in range(w[id(g)]):
                            try:
                                next(g)
                                yield
                            except StopIteration:
                                gens.remove(g)
                                break

            run(emit_block(0, 'front_gla'))
            for blk in range(NBLK):
                if blk + 1 < NBLK:
                    P.dma("pool", "d_xT%d" % ((blk + 1) % 2), xTb[(blk + 1) % 2][:, :, :],
                          xT_v[:, :, (blk + 1) * TB:(blk + 2) * TB], writes=[("xTb", (blk + 1) % 2)])
                if blk == 0:
                    run(emit_block(0, 'rw_front'))

                def side(blk=blk):
                    yield from emit_block(blk, 'gla_core')
                    if blk + 1 < NBLK:
                        yield from emit_block(blk + 1, 'rw_front')
                        yield from emit_block(blk + 1, 'front_gla')
                run(ileave(emit_block(blk, 'chunks'), side(), weights=SW))


        def layer_norm(eng2, src, dst, gbc, bbc, stt, mvt, keys):
            ksrc, kdst = keys
            for hh in range(2):
                P.op("dve", C("bn_stats", stt[:, hh, :], src[:, hh * 512:(hh + 1) * 512]), reads=[ksrc], writes=["ln_st"])
            P.op("dve", C("bn_aggr", mvt[:, 0:2], stt[:, :, :].rearrange("p a b -> p (a b)")), reads=["ln_st"], writes=["ln_mv"])
            P.op("act", C("activation", out=mvt[:, 2:3], in_=mvt[:, 1:2], func=AF.Ln, bias=1e-5, scale=1.0), reads=["ln_mv"], writes=["ln_mv"])
            P.op("act", C("activation", out=mvt[:, 2:3], in_=mvt[:, 2:3], func=AF.Exp, scale=-0.5), reads=["ln_mv"], writes=["ln_mv"])
            P.op("dve", C("tensor_scalar", src[:, :], src[:, :], mvt[:, 0:1], mvt[:, 2:3], ALU.subtract, ALU.mult), reads=[ksrc, "ln_mv"], writes=[ksrc])
            P.op(eng2, C("tensor_tensor", out=src[:, :], in0=src[:, :], in1=gbc[:, :], op=ALU.mult), reads=[ksrc, "lnp"], writes=[ksrc])
            P.op(eng2, C("tensor_tensor", out=dst[:, :], in0=src[:, :], in1=bbc[:, :], op=ALU.add), reads=[ksrc, "lnp"], writes=[kdst])

        if phases >= 2:
          P.barrier()
          with contextlib.ExitStack() as sbc:
            w1sb = sbt(sbc, "w1sb", [128, KC, DFF], BF16)
            lnst = sbt(sbc, "lnst", [128, 2, 6], F32)
            lnmv = sbt(sbc, "lnmv", [128, 4], F32)
            w1_v = w1_d.rearrange("(kc p) n -> p kc n", p=128)
            with contextlib.ExitStack() as sb_:
                wmg = sbt(sb_, "wmg", [128, KC, 2 * D], BF16)
                wbr = sbt(sb_, "wbr", [128, KC, D], BF16)
                wout = sbt(sb_, "wout", [128, KC, D], BF16)
                g1bc = sbt(sb_, "g1bc", [128, D], F32)
                b1bc = sbt(sb_, "b1bc", [128, D], F32)
                xTb2 = [sbt(sb_, "xTb2_%d" % i, [128, KC, TB], BF16) for i in range(2)]
                oTb = [sbt(sb_, "oTb%d" % i, [128, KC, TB], BF16) for i in range(2)]
                gate = sbt(sb_, "gate", [128, 16, TB], BF16)
                m32 = sbt(sb_, "m32", [128, TB], F32)
                tg = sbt(sb_, "tg", [128, TB], F32)
                mT = sbt(sb_, "mT", [128, KC, TB], BF16)
                xtok = [sbt(sb_, "xtok%d" % i, [128, D], F32) for i in range(4)]
                rsd = [sbt(sb_, "rsd%d" % i, [128, D], F32) for i in range(2)]
                w_mg_v = w_mg_d.rearrange("(kc p) n -> p kc n", p=128)
                w_br_v = w_br_d.rearrange("(kc p) n -> p kc n", p=128)
                w_out_v = w_out_d.rearrange("(kc p) n -> p kc n", p=128)
                P.dma("pool", "d_xB0", xTb2[0][:, :, :], xT_v[:, :, 0:TB], writes=[("xTb2", 0)])
                for g in range(4):
                    P.dma("pool", "d_wmg%d" % g, wmg[:, :, g * 512:(g + 1) * 512], w_mg_v[:, :, g * 512:(g + 1) * 512], writes=[("wmg", g)])
                for g in range(2):
                    P.dma("pool", "d_wbr%d" % g, wbr[:, :, g * 512:(g + 1) * 512], w_br_v[:, :, g * 512:(g + 1) * 512], writes=[("wbr", g)])
                for g in range(2):
                    P.dma("pool", "d_wout%d" % g, wout[:, :, g * 512:(g + 1) * 512], w_out_v[:, :, g * 512:(g + 1) * 512], writes=[("wout", g)])
                w1_issued = False

                def b_loads(bb):
                    P.dma("sp", "d_oTb%d" % (bb % 2), oTb[bb % 2][:, :, :], oT_v[:, :, bb * TB:(bb + 1) * TB],
                          reads=[("oTd", bb, 0), ("oTd", bb, 1)], writes=[("oTb", bb % 2)])
                    for jj in range(TB // 128):
                        tt_ = bb * (TB // 128) + jj
                        P.dma("sp", "d_xtok%d" % (tt_ % 4), xtok[tt_ % 4][:, :], x_d[tt_ * 128:(tt_ + 1) * 128, :], writes=[("xtok", tt_ % 4)])

                for blk in range(NBLK):
                    xs = blk % 2
                    t0 = blk * TB
                    if blk + 1 < NBLK:
                        P.dma("pool", "d_xB%d" % ((blk + 1) % 2), xTb2[(blk + 1) % 2][:, :, :], xT_v[:, :, t0 + TB:t0 + 2 * TB],
                              writes=[("xTb2", (blk + 1) % 2)])
                    if not w1_issued:
                        for g in range(8):
                            P.dma("pool", "d_w1_%d" % g, w1sb[:, :, g * 512:(g + 1) * 512], w1_v[:, :, g * 512:(g + 1) * 512], writes=[("w1", g)])
                        w1_issued = True
                    if blk == 0:
                        b_loads(0)
                        P.dma("sp", "d_ln1g", g1bc[:, :], rows_d[0:1, R_LN1G:R_LN1G + D].to_broadcast([128, D]), writes=["lnp"])
                        P.dma("sp", "d_ln1b", b1bc[:, :], rows_d[0:1, R_LN1B:R_LN1B + D].to_broadcast([128, D]), writes=["lnp"])
                    if blk + 1 < NBLK:
                        b_loads(blk + 1)
                    for c in range(16):
                        bank, pk = psum()
                        mm_group(bank[:, 0:TB], [(wmg[:, kc, c * 128:(c + 1) * 128], xTb2[xs][:, kc, :]) for kc in range(KC)], pk,
                                 reads=[("wmg", c // 4), ("xTb2", xs)])
                        P.op("act", C("activation", out=gate[:, c, :], in_=bank[:, 0:TB], func=AF.Sigmoid, bias=ppc(PP_BM, c), scale=1.0),
                             reads=[pk, "pp"], writes=[("gate", c)])
                    for c in range(8):
                        ba, ka = psum()
                        mm_group(ba[:, 0:TB], [(wbr[:, kc, c * 128:(c + 1) * 128], oTb[xs][:, kc, :]) for kc in range(0, 4)], ka,
                                 reads=[("wbr", c // 4), ("oTb", xs)])
                        bb, kb = psum()
                        mm_group(bb[:, 0:TB], [(wbr[:, kc, c * 128:(c + 1) * 128], oTb[xs][:, kc, :]) for kc in range(4, 8)], kb,
                                 reads=[("wbr", c // 4), ("oTb", xs)])
                        P.op("dve", C("tensor_tensor", out=m32[:, :], in0=ba[:, 0:TB], in1=gate[:, c, :], op=ALU.mult), reads=[ka, ("gate", c)], writes=["m32"])
                        P.op("dve", C("tensor_tensor", out=tg[:, :], in0=bb[:, 0:TB], in1=gate[:, 8 + c, :], op=ALU.mult), reads=[kb, ("gate", 8 + c)], writes=["tg"])
                        P.op("dve", C("tensor_tensor", out=mT[:, c, :], in0=m32[:, :], in1=tg[:, :], op=ALU.add), reads=["m32", "tg"], writes=[("mT", c)])
                    for j in range(TB // 128):
                        ti = blk * (TB // 128) + j
                        sl = ti % 2
                        for hh in range(2):
                            bank, pk = psum()
                            mm_group(bank[:, :], [(mT[:, kc, j * 128:(j + 1) * 128], wout[:, kc, hh * 512:(hh + 1) * 512]) for kc in range(KC)], pk,
                                     reads=[("wout", hh)] + [("mT", c) for c in range(8)])
                            P.op("dve", C("scalar_tensor_tensor", out=rsd[sl][:, hh * 512:(hh + 1) * 512], in0=xtok[ti % 4][:, hh * 512:(hh + 1) * 512],
                                          scalar=ALPHA, in1=bank[:, :], op0=ALU.mult, op1=ALU.add), reads=[pk, ("xtok", ti % 4)], writes=[("rsd", sl)])
                        layer_norm("pool", rsd[sl], rsd[sl], g1bc, b1bc, lnst, lnmv, (("rsd", sl), ("rsd", sl)))
                        P.dma("sp", "d_scrX%d" % sl, x1_d[ti * 128:(ti + 1) * 128, :], rsd[sl][:, :], reads=[("rsd", sl)], writes=[("x1d", ti)])

            if phases >= 3:
              P.barrier()
              with contextlib.ExitStack() as sc_:
                w2sb = sbt(sc_, "w2sb", [128, 32, D], BF16)
                g2bc = sbt(sc_, "g2bc", [128, D], F32)
                b2lbc = sbt(sc_, "b2lbc", [128, D], F32)
                bdbc = sbt(sc_, "bdbc", [128, D], F32)
                x1tok = [sbt(sc_, "x1tok%d" % i, [128, D], F32) for i in range(4)]
                x1bf = [sbt(sc_, "x1bf%d" % i, [128, D], BF16) for i in range(2)]
                x1T = sbt(sc_, "x1T", [128, KC, TB], BF16)
                rl = [sbt(sc_, "rl%d" % i, [128, TB], F32) for i in range(2)]
                hT = sbt(sc_, "hT", [128, 32, TB], BF16)
                rs2 = [sbt(sc_, "rs2_%d" % i, [128, D], F32) for i in range(2)]
                w2_v = w2_d.rearrange("(f p) n -> p f n", p=128)
                for g in range(4):
                    P.dma("pool", "d_w2_%d" % g, w2sb[:, g * 8:(g + 1) * 8, :], w2_v[:, g * 8:(g + 1) * 8, :], writes=[("w2", g)])
                def c_loads(bb):
                    for jj in range(TB // 128):
                        tt_ = bb * (TB // 128) + jj
                        P.dma("sp", "d_x1t%d" % (tt_ % 4), x1tok[tt_ % 4][:, :], x1_d[tt_ * 128:(tt_ + 1) * 128, :], reads=[("x1d", tt_)],
                              writes=[("x1tok", tt_ % 4)])

                def c_tr(bb):
                    for jj in range(TB // 128):
                        tt_ = bb * (TB // 128) + jj
                        P.op("act", C("activation", out=x1bf[jj][:, :], in_=x1tok[tt_ % 4][:, :], func=AF.Copy), reads=[("x1tok", tt_ % 4)], writes=[("x1bf", jj)])
                        bt, kt = psum()
                        btv = bfv(bt)
                        for kc in range(KC):
                            P.op("pe", C("transpose", btv[:, kc * 128:(kc + 1) * 128], x1bf[jj][:, kc * 128:(kc + 1) * 128], ident[:, :]),
                                 reads=[("x1bf", jj), "ident"], writes=[kt], inc=(kc == KC - 1))
                        P.op("dve", C("tensor_copy", x1T[:, :, jj * 128:(jj + 1) * 128], btv[:, :].rearrange("p (kc t) -> p kc t", kc=KC)),
                             reads=[kt], writes=[("x1T", jj)])

                c_loads(0)
                P.dma("sp", "d_ln2g", g2bc[:, :], rows_d[0:1, R_LN2G:R_LN2G + D].to_broadcast([128, D]), writes=["lnp"])
                P.dma("sp", "d_ln2b", b2lbc[:, :], rows_d[0:1, R_LN2B:R_LN2B + D].to_broadcast([128, D]), writes=["lnp"])
                P.dma("sp", "d_bd", bdbc[:, :], rows_d[0:1, R_B2:R_B2 + D].to_broadcast([128, D]), writes=["bdbc"])
                c_tr(0)
                for blk in range(NBLK):
                    if blk + 1 < NBLK:
                        c_loads(blk + 1)
                    for f in range(32):
                        bank, pk = psum()
                        mm_group(bank[:, 0:TB], [(w1sb[:, kc, f * 128:(f + 1) * 128], x1T[:, kc, :]) for kc in range(KC)], pk,
                                 reads=[("w1", f // 4)] + [("x1T", j) for j in range(TB // 128)])
                        rs = f % 2
                        P.op("act", C("activation", out=rl[rs][:, :], in_=bank[:, 0:TB], func=AF.Relu, bias=ppc(PP_B1, f), scale=1.0),
                             reads=[pk, "pp"], writes=[("rl", rs)])
                        P.op("dve", C("tensor_tensor", out=hT[:, f, :], in0=rl[rs][:, :], in1=rl[rs][:, :], op=ALU.mult), reads=[("rl", rs)], writes=[("hT", f)])
                    if blk + 1 < NBLK:
                        c_tr(blk + 1)
                    for j in range(TB // 128):
                        ti = blk * (TB // 128) + j
                        sl = ti % 2
                        for hh in range(2):
                            bank, pk = psum()
                            mm_group(bank[:, :], [(hT[:, f, j * 128:(j + 1) * 128], w2sb[:, f, hh * 512:(hh + 1) * 512]) for f in range(32)], pk,
                                     reads=[("w2", g) for g in range(4)] + [("hT", f) for f in range(32)])
                            P.op("dve", C("scalar_tensor_tensor", out=rs2[sl][:, hh * 512:(hh + 1) * 512], in0=x1tok[ti % 4][:, hh * 512:(hh + 1) * 512],
                                          scalar=ALPHA, in1=bank[:, :], op0=ALU.mult, op1=ALU.add), reads=[pk, ("x1tok", ti % 4)], writes=[("rs2", sl)])
                        P.op("pool", C("tensor_tensor", out=rs2[sl][:, :], in0=rs2[sl][:, :], in1=bdbc[:, :], op=ALU.add), reads=[("rs2", sl), "bdbc"], writes=[("rs2", sl)])
                        layer_norm("pool", rs2[sl], rs2[sl], g2bc, b2lbc, lnst, lnmv, (("rs2", sl), ("rs2", sl)))
                        P.dma("sp", "d_out%d" % sl, out_d[ti * 128:(ti + 1) * 128, :], rs2[sl][:, :], reads=[("rs2", sl)], writes=[("outd", ti)])

        P.final_wait("sp", [k for k in P.dcnt if k.startswith("d_out") or k.startswith("d_scr")])
        P.emit()
    return nc


def prep_shared(inp):
    f = lambda a: np.ascontiguousarray(np.asarray(a, dtype=np.float32))
    L = 0

    def fm(v, n):
        return f(v).reshape(n, 128).T

    pp = np.concatenate([
        fm(inp["mu_shift"][L], 14), fm(inp["b_merge"][L], 16), fm(inp["b_mlp_up"][L], 32),
        fm(inp["rwkv_w0"][L], 4), fm(inp["rwkv_a0"][L], 4), fm(inp["rwkv_k_k"][L], 4), fm(inp["rwkv_k_a"][L], 4),
        fm(f(inp["rwkv_r_k"][L]).reshape(512), 4), fm(inp["rwkv_gn_w"][L], 4), fm(inp["rwkv_gn_b"][L], 4),
        fm(inp["b_gk"][L], 2)], axis=1)
    rows = np.concatenate([f(inp["ln1_g"][L]), f(inp["ln1_b"][L]), f(inp["ln2_g"][L]), f(inp["ln2_b"][L]),
                           f(inp["b_mlp_down"][L]), f(inp["gla_norm_w"][L])])[None, :]
    return {
        "w_in": f(inp["w_in"][L]), "w_merge": f(inp["w_merge"][L]),
        "w_branch": f(inp["w_branch"][L]).reshape(1024, 1024), "w_out": f(inp["w_out"][L]),
        "w1": f(inp["w_mlp_up"][L]), "w2": f(inp["w_mlp_down"][L]),
        "pp": f(pp), "rows": f(rows), "w_gk_up": f(inp["w_gk_up"][L]),
        "wa_up": f(np.concatenate([f(inp["rwkv_w_up"][L]), f(inp["rwkv_a_up"][L])], axis=0)),
        "g_up": f(inp["rwkv_g_up"][L]),
    }


def prep_core(shared, xb):
    m = dict(shared)
    m["x"] = np.ascontiguousarray(xb, dtype=np.float32)
    m["xT"] = np.ascontiguousarray(np.asarray(xb, dtype=np.float32).T)
    return m


_NC_CACHE = {}


def kernel(**inputs):
    x = np.asarray(inputs["x"], dtype=np.float32)
    B, T, _ = x.shape
    shared = prep_shared(inputs)
    if T not in _NC_CACHE:
        _NC_CACHE[T] = build(T)
    nc = _NC_CACHE[T]
    in_maps = [prep_core(shared, x[b]) for b in range(B)]
    res = run_bass_kernel_spmd(nc, in_maps, core_ids=list(range(B)))
    return np.stack([np.asarray(r["out"], dtype=np.float32) for r in res.results], axis=0)
```
